# Optimizing a Trainium2 kernel written in Bass

```python
import math
import jax, jax.numpy as jnp
from jax import lax
import numpy as np

D_MODEL = 1024
BATCH = 8
SEQ = 2048
DEPTH = 1
DEC_BATCH = 128
DEC_SEQ = 8
PAST_LEN = 16384
PAGE_SIZE = 128

D_MIX = D_MODEL
POOL_WIDTH = D_MIX // 2
POOL_WINDOWS = (2, 4, 8, 16)
POOL_GROUPS = len(POOL_WINDOWS)
POOL_GROUP_DIM = POOL_WIDTH // POOL_GROUPS
POOL_BUF = max(POOL_WINDOWS) - 1
GLA_WIDTH = D_MIX - POOL_WIDTH
GLA_HEADS = 4
GLA_DV = GLA_WIDTH // GLA_HEADS
GLA_DK = GLA_DV // 2
GLA_KEY_WIDTH = GLA_HEADS * GLA_DK
GLA_RANK = 16
GLA_TAU = 16.0
GLA_CHUNK = 64
LN_EPS = 1e-5
RMS_EPS = 1e-6
DEEPNORM_ALPHA = (2.0 * DEPTH) ** 0.25
DEEPNORM_BETA = (8.0 * DEPTH) ** -0.25
SPLITS = (POOL_WIDTH, POOL_WIDTH, GLA_KEY_WIDTH, GLA_KEY_WIDTH, GLA_WIDTH, GLA_WIDTH, GLA_RANK)
D_IN = sum(SPLITS)
SPLIT_IDX = [int(i) for i in np.cumsum(SPLITS)[:-1]]

kernel_name = "hybrid_pool_gla_deepnorm_step"


def pool_mix(u, buf, start_pos, w_map, scale):
    B, T, C = u.shape
    ext = jnp.concatenate([buf.astype(u.dtype), u], axis=1)
    ext32 = ext.astype(jnp.float32)
    cs = jnp.concatenate([jnp.zeros((B, 1, C), jnp.float32), jnp.cumsum(ext32, axis=1)], axis=1)
    top = cs[:, POOL_BUF + 1:POOL_BUF + 1 + T]
    pos = start_pos + jnp.arange(T, dtype=jnp.int32)
    self_rows = ext32[:, POOL_BUF:]
    outs = []
    for g, w in enumerate(POOL_WINDOWS):
        sl = slice(g * POOL_GROUP_DIM, (g + 1) * POOL_GROUP_DIM)
        lo = cs[:, POOL_BUF + 1 - w:POOL_BUF + 1 - w + T, sl]
        cnt = jnp.minimum(pos + 1, w).astype(jnp.float32)[None, :, None]
        outs.append((top[..., sl] - lo) / cnt - self_rows[..., sl])
    pooled = jnp.stack(outs, axis=2)
    mixed = jnp.einsum('btgc,gce->btge', pooled, w_map.astype(jnp.float32)).reshape(B, T, C)
    mixed = mixed * scale.astype(jnp.float32)
    return mixed.astype(u.dtype), ext[:, -POOL_BUF:]


def gla_chunked(q, k, v, log_a, S0):
    B, T, H, DK = q.shape
    DV = v.shape[-1]
    C = GLA_CHUNK if T % GLA_CHUNK == 0 else T
    N = T // C

    def to_chunks(a):
        return a.astype(jnp.float32).reshape(B, N, C, H, a.shape[-1]).transpose(1, 0, 3, 2, 4)

    qc, kc, vc, gc = to_chunks(q), to_chunks(k), to_chunks(v), to_chunks(log_a)
    mask = jnp.tril(jnp.ones((C, C), dtype=bool))[None, None, :, :, None]

    def step(S, inp):
        qb, kb, vb, gb = inp
        b = jnp.cumsum(gb, axis=2)
        inter = jnp.einsum('bhck,bhkv->bhcv', qb * jnp.exp(b), S)
        diff = b[:, :, :, None, :] - b[:, :, None, :, :]
        decay = jnp.exp(jnp.where(mask, diff, -jnp.inf))
        A = jnp.einsum('bhtk,bhsk,bhtsk->bhts', qb, kb, decay)
        intra = jnp.einsum('bhts,bhsv->bhtv', A, vb)
        bC = b[:, :, -1]
        S_new = jnp.exp(bC)[..., None] * S + jnp.einsum('bhsk,bhsv->bhkv', kb * jnp.exp(bC[:, :, None] - b), vb)
        return S_new, inter + intra

    S_fin, o = lax.scan(step, S0.astype(jnp.float32), (qc, kc, vc, gc))
    o = o.transpose(1, 0, 3, 2, 4).reshape(B, T, H, DV)
    return o, S_fin


def layer(x, buf, S, start_pos, w_in, w_map, pool_scale, w_a2, b_a, norm_g, w_out, ln_g, ln_b):
    B, T, _ = x.shape
    proj = jnp.einsum('btd,de->bte', x, w_in)
    u, gp, q, k, v, gg, fa = jnp.split(proj, SPLIT_IDX, axis=-1)
    pool_out, new_buf = pool_mix(u, buf, start_pos, w_map, pool_scale)
    g_logit = jnp.einsum('btr,rk->btk', fa, w_a2) + b_a
    log_a = jax.nn.log_sigmoid(g_logit.astype(jnp.float32)) / GLA_TAU
    qh = q.reshape(B, T, GLA_HEADS, GLA_DK) * (GLA_DK ** -0.5)
    kh = k.reshape(B, T, GLA_HEADS, GLA_DK)
    vh = v.reshape(B, T, GLA_HEADS, GLA_DV)
    o, S_new = gla_chunked(qh, kh, vh, log_a.reshape(B, T, GLA_HEADS, GLA_DK), S)
    o = o * lax.rsqrt(jnp.mean(o * o, axis=-1, keepdims=True) + RMS_EPS)
    o = (o * norm_g.astype(jnp.float32).reshape(GLA_HEADS, GLA_DV)).reshape(B, T, GLA_WIDTH)
    mix = jnp.concatenate([pool_out * jax.nn.silu(gp), o.astype(x.dtype) * jax.nn.silu(gg)], axis=-1)
    y = jnp.einsum('bte,ed->btd', mix, w_out)
    h = (DEEPNORM_ALPHA * x + y).astype(jnp.float32)
    mu = jnp.mean(h, axis=-1, keepdims=True)
    var = jnp.mean(jnp.square(h - mu), axis=-1, keepdims=True)
    out = (h - mu) * lax.rsqrt(var + LN_EPS) * ln_g.astype(jnp.float32) + ln_b.astype(jnp.float32)
    return out.astype(x.dtype), new_buf, S_new.astype(S.dtype)


def setup_inputs(seed: int = 0) -> dict:
    key = jax.random.key(seed)
    ks = jax.random.split(key, 13)
    f32 = jnp.float32
    x_prompt = jax.random.normal(ks[0], (BATCH, SEQ, D_MODEL), f32)
    x_sample = jax.random.normal(ks[1], (DEC_BATCH, DEC_SEQ, D_MODEL), f32)
    state_pool = jax.random.normal(ks[2], (DEPTH, DEC_BATCH, POOL_BUF, POOL_WIDTH), f32)
    state_gla = 0.5 * jax.random.normal(ks[3], (DEPTH, DEC_BATCH, GLA_HEADS, GLA_DK, GLA_DV), f32)
    w_in = jax.random.normal(ks[4], (DEPTH, D_MODEL, D_IN), f32) * D_MODEL ** -0.5
    w_pool_map = jax.random.normal(ks[5], (DEPTH, POOL_GROUPS, POOL_GROUP_DIM, POOL_GROUP_DIM), f32) * POOL_GROUP_DIM ** -0.5
    pool_scale = 1.0 + 0.02 * jax.random.normal(ks[6], (DEPTH, POOL_WIDTH), f32)
    w_gla_a2 = jax.random.normal(ks[7], (DEPTH, GLA_RANK, GLA_KEY_WIDTH), f32) * GLA_RANK ** -0.5
    b_gla_a = 0.01 * jax.random.normal(ks[8], (DEPTH, GLA_KEY_WIDTH), f32)
    gla_norm_g = 1.0 + 0.02 * jax.random.normal(ks[9], (DEPTH, GLA_WIDTH), f32)
    w_out = jax.random.normal(ks[10], (DEPTH, D_MIX, D_MODEL), f32) * (D_MIX ** -0.5) * DEEPNORM_BETA
    ln_g = 1.0 + 0.02 * jax.random.normal(ks[11], (DEPTH, D_MODEL), f32)
    ln_b = 0.02 * jax.random.normal(ks[12], (DEPTH, D_MODEL), f32)
    return {"x_prompt": x_prompt, "x_sample": x_sample, "state_pool": state_pool, "state_gla": state_gla,
            "w_in": w_in, "w_pool_map": w_pool_map, "pool_scale": pool_scale, "w_gla_a2": w_gla_a2,
            "b_gla_a": b_gla_a, "gla_norm_g": gla_norm_g, "w_out": w_out, "ln_g": ln_g, "ln_b": ln_b}


def reference(x_prompt, x_sample, state_pool, state_gla, w_in, w_pool_map, pool_scale, w_gla_a2,
              b_gla_a, gla_norm_g, w_out, ln_g, ln_b):
    hp, hs = x_prompt, x_sample
    pool_p, gla_p, pool_s, gla_s = [], [], [], []
    for l in range(DEPTH):
        params = (w_in[l], w_pool_map[l], pool_scale[l], w_gla_a2[l], b_gla_a[l], gla_norm_g[l],
                  w_out[l], ln_g[l], ln_b[l])
        buf0 = jnp.zeros((BATCH, POOL_BUF, POOL_WIDTH), state_pool.dtype)
        S0 = jnp.zeros((BATCH, GLA_HEADS, GLA_DK, GLA_DV), state_gla.dtype)
        hp, bp, Sp = layer(hp, buf0, S0, 0, *params)
        hs, bs, Ss = layer(hs, state_pool[l], state_gla[l], PAST_LEN, *params)
        pool_p.append(bp)
        gla_p.append(Sp)
        pool_s.append(bs)
        gla_s.append(Ss)
    new_pool_prompt = jnp.stack(pool_p, axis=0)
    new_gla_prompt = jnp.stack(gla_p, axis=0)
    new_pool_sample = jnp.stack(pool_s, axis=0)
    new_gla_sample = jnp.stack(gla_s, axis=0)
    return (hp, hs, new_pool_prompt, new_gla_prompt, new_pool_sample, new_gla_sample)
```

```python
from contextlib import ExitStack

import numpy as np
import concourse.bass as bass
import concourse.mybir as mybir
from concourse.bass_utils import run_bass_kernel_spmd

F32 = mybir.dt.float32
BF16 = mybir.dt.bfloat16
AF = mybir.ActivationFunctionType
ALU = mybir.AluOpType

NCORES = 8
NTOK = 2176
NTILE = 17
WINS = (2, 4, 8, 16)
ALPHA = 2.0 ** 0.25
LN_EPS = 1e-5
RMS_EPS = 1e-6


def build_consts():
    c = {}
    c["ident"] = np.eye(128, dtype=np.float32)
    s = np.arange(128)[:, None]
    t = np.arange(128)[None, :]
    tri_p = np.where(s <= t, -1.0 / 16.0, 0.0).astype(np.float32)
    same = (s // 8) == (t // 8)
    tri_s = np.where(same & (s <= t), -1.0 / 16.0, 0.0).astype(np.float32)
    c["tri"] = np.ascontiguousarray(np.stack([tri_p, tri_s], axis=1))
    mask_p = (s <= t).astype(np.float32)
    mask_s = (same & (s <= t)).astype(np.float32)
    c["mask"] = np.ascontiguousarray(np.stack([mask_p, mask_s], axis=1))
    bands = np.zeros((128, 4, 6, 128), np.float32)
    for g, w in enumerate(WINS):
        bands[:, g, 0, :] = np.where((s <= t) & (s >= t - w + 1), 1.0 / w, 0.0) - (s == t)
        bands[:, g, 1, :] = np.where((s - 128) >= (t - w + 1), 1.0 / w, 0.0)
        cnt = np.minimum(t + 1, w)
        bands[:, g, 2, :] = np.where((s <= t) & (s >= t - w + 1), 1.0 / cnt, 0.0) - (s == t)
        ts = t % 8
        ss = s % 8
        bands[:, g, 3, :] = np.where(same & (ss <= ts) & (ss >= ts - w + 1), 1.0 / w, 0.0) - (s == t)
        for hh in range(2):
            m = np.zeros((128, 128), np.float32)
            for r in range(120):
                sl, j = divmod(r, 15)
                seq = 8 * hh + sl
                for tt in range(8):
                    if j >= 15 + tt - w + 1:
                        m[r, seq * 8 + tt] = 1.0 / w
            bands[:, g, 4 + hh, :] = m
    c["bands"] = bands
    c["ind"] = ((np.arange(128)[:, None] // 8) == np.arange(16)[None, :]).astype(np.float32)
    return c


class Buf:
    __slots__ = ("name", "w", "r", "psum")

    def __init__(self, name, psum=False):
        self.name = name
        self.w = None
        self.r = {}
        self.psum = psum


class Eng:
    def __init__(self, name, h, sem):
        self.name = name
        self.h = h
        self.sem = sem
        self.cnt = 0
        self.seen = {}
        self.dsems = []
        self.dnext = 0


class Tracker:
    NDSEM = 10

    def __init__(self, nc, stack, dma_queues=("sp", "act", "pool")):
        self.nc = nc
        self.eng = {}
        for name, h in (("pe", nc.tensor), ("act", nc.scalar), ("dve", nc.vector),
                        ("pool", nc.gpsimd), ("sp", nc.sync)):
            sem = stack.enter_context(nc.semaphore("s_" + name))
            self.eng[name] = Eng(name, h, sem)
        for q in dma_queues:
            e = self.eng[q]
            for i in range(self.NDSEM):
                s = stack.enter_context(nc.semaphore("d_%s%d" % (q, i)))
                e.dsems.append([s, 0])

    def _need(self, e, dep):
        if dep is None:
            return
        if dep[0] == "m":
            for d in dep[1]:
                self._need(e, d)
            return
        if dep[0] == "d":
            _, sem, val = dep
            key = ("d", sem.num)
            if e.seen.get(key, 0) >= val:
                return
            e.h.wait_ge(sem, val)
            e.seen[key] = val
            return
        _, oname, cnt = dep
        if e.seen.get(oname, 0) >= cnt:
            return
        e.h.wait_ge(self.eng[oname].sem, cnt)
        e.seen[oname] = cnt

    def op(self, ename, fn, reads=(), writes=()):
        e = self.eng[ename]
        for b in reads:
            self._need(e, b.w)
            if b.psum:
                for k, d in b.r.items():
                    if not (d[0] == "e" and d[1] == ename):
                        self._need(e, d)
        for b in writes:
            self._need(e, b.w)
            for k, d in b.r.items():
                self._need(e, d)
        ins = fn()
        ins.then_inc(e.sem, 1)
        e.cnt += 1
        dep = ("e", ename, e.cnt)
        for b in reads:
            b.r[ename] = dep
        for b in writes:
            b.w = dep
            b.r = {}
        return ins

    def dma(self, ename, out, in_, reads=(), writes=()):
        e = self.eng[ename]
        for b in reads:
            self._need(e, b.w)
        for b in writes:
            self._need(e, b.w)
            for k, d in b.r.items():
                self._need(e, d)
        slot = e.dsems[e.dnext % len(e.dsems)]
        e.dnext += 1
        sem, val = slot
        if val > 0:
            self._need(e, ("d", sem, val))
        e.h.dma_start(out=out, in_=in_).then_inc(sem, 16)
        slot[1] = val + 16
        dep = ("d", sem, val + 16)
        for b in writes:
            b.w = dep
            b.r = {}
        for b in reads:
            b.r[("d", sem.num)] = dep
        return dep

    def dma_batch(self, ename, pairs, reads=(), writes=()):
        e = self.eng[ename]
        for b in reads:
            self._need(e, b.w)
        for b in writes:
            self._need(e, b.w)
            for k, d in b.r.items():
                self._need(e, d)
        deps = []
        for out, in_ in pairs:
            slot = e.dsems[e.dnext % len(e.dsems)]
            e.dnext += 1
            sem, val = slot
            if val > 0:
                self._need(e, ("d", sem, val))
            e.h.dma_start(out=out, in_=in_).then_inc(sem, 16)
            slot[1] = val + 16
            deps.append(("d", sem, val + 16))
        dep = ("m", tuple(deps))
        for b in writes:
            b.w = dep
            b.r = {}
        for b in reads:
            b.r[("m", deps[0][1].num, deps[0][2])] = dep
        return dep

    def wait_all_dma(self, ename):
        e = self.eng[ename]
        for q in self.eng.values():
            for sem, val in q.dsems:
                if val > 0:
                    self._need(e, ("d", sem, val))


class TB:
    __slots__ = ("t", "b")

    def __init__(self, t, name, psum=False):
        self.t = t
        self.b = Buf(name, psum)


def build_program():
    nc = bass.Bass("TRN2", target_bir_lowering=False)
    dt_in = lambda name, shape: nc.dram_tensor(name, shape, F32, kind="ExternalInput").ap()
    dt_out = lambda name, shape: nc.dram_tensor(name, shape, F32, kind="ExternalOutput").ap()
    x_d = dt_in("x", [NTOK, 1024])
    spool_d = dt_in("spool", [16, 15, 512])
    sgla_d = dt_in("sgla", [16, 4, 64, 128])
    w_in_d = dt_in("w_in", [1024, 2576])
    w_map_d = dt_in("w_map", [4, 128, 128])
    pscale_d = dt_in("pscale", [128, 4])
    w_a2_d = dt_in("w_a2", [16, 256])
    b_a_d = dt_in("b_a", [1, 256])
    ng_d = dt_in("ng", [128, 4])
    w_out_d = dt_in("w_out", [1024, 1024])
    ln_g_d = dt_in("ln_g", [1, 1024])
    ln_b_d = dt_in("ln_b", [1, 1024])
    ident_d = dt_in("ident", [128, 128])
    tri_d = dt_in("tri", [128, 2, 128])
    mask_d = dt_in("mask", [128, 2, 128])
    bands_d = dt_in("bands", [128, 4, 6, 128])
    ind_d = dt_in("ind", [128, 16])
    y_d = dt_out("y", [NTOK, 1024])
    pool_p_d = dt_out("pool_p", [15, 512])
    gla_p_d = dt_out("gla_p", [4, 64, 128])
    pool_s_d = dt_out("pool_s", [16, 15, 512])
    gla_s_d = dt_out("gla_s", [16, 4, 64, 128])

    with ExitStack() as st:
        T = Tracker(nc, st)
        V = nc.vector
        A = nc.scalar
        G = nc.gpsimd
        PE = nc.tensor

        def sb(name, shape, dt=F32):
            return TB(st.enter_context(nc.sbuf_tensor(name, shape, dt)), name)

        def ring(name, n, shape, dt=F32):
            return [sb("%s%d" % (name, i), shape, dt) for i in range(n)]

        pbig = [TB(st.enter_context(nc.psum_tensor("pbig", [128, 1024], F32)), "pbig", True)]
        banks = [TB(st.enter_context(nc.psum_tensor("bank%d" % i, [128, 512], F32)), "bank%d" % i, True) for i in range(6)]
        bank_i = [0]

        def nbank():
            b = banks[bank_i[0] % len(banks)]
            bank_i[0] += 1
            return b

        w_in_bf = sb("w_in_bf", [128, 8, 2576], BF16)
        w_out_bf = sb("w_out_bf", [128, 8, 1024], BF16)
        stage_t = st.enter_context(nc.sbuf_tensor("stage", [128, 2, 1024], F32))
        stage_b = [Buf("stage0"), Buf("stage1")]
        w_map_bf = sb("w_map_bf", [128, 4, 128], BF16)
        w_a2_bf = sb("w_a2_bf", [128, 256], BF16)
        pscale = sb("pscale_sb", [128, 4])
        ng = sb("ng_sb", [128, 4])
        ln_g = sb("ln_g_sb", [128, 1024])
        ln_b = sb("ln_b_sb", [128, 1024])
        ident = sb("ident_sb", [128, 128])
        tri = sb("tri_sb", [128, 2, 128])
        mask = sb("mask_sb", [128, 2, 128])
        bands = sb("bands_bf", [128, 4, 6, 128], BF16)
        ind = sb("ind_sb", [128, 16])
        ones_bf = sb("ones_bf", [128, 128], BF16)
        neghalf = sb("neghalf", [128, 2])
        bufS_bf = sb("bufS_bf", [128, 2, 512], BF16)
        S0b = sb("S0b", [128, 16, 128])
        S0b_h = [Buf("S0b_h0"), Buf("S0b_h1")]
        S = sb("S", [128, 2, 128])
        S_bf = ring("S_bf", 2, [128, 2, 128], BF16)

        NX = 4
        xa = ring("xa", NX, [128, 1024])
        xT = ring("xT", 1, [128, 8, 512], BF16)
        faT = ring("faT", 1, [128, 512], BF16)
        qT = ring("qT", 1, [128, 2, 512])
        kT = ring("kT", 1, [128, 2, 512])
        sgp = ring("sgp", 1, [128, 4, 512])
        sgg = ring("sgg", 1, [128, 4, 512])
        u_bf = ring("u_bf", 5, [128, 512], BF16)
        v_bf = ring("v_bf", 4, [128, 512], BF16)
        pooledT = ring("pooledT", 1, [128, 4, 512], BF16)
        mixT = ring("mixT", 1, [128, 8, 512], BF16)
        mixTg = [Buf("mixTg0")]
        mixTg4 = [Buf("mixTg4_%d" % i) for i in range(4)]
        pooledT_g = [Buf("pooledT_g%d" % i) for i in range(4)]
        expg = ring("expg", 1, [128, 512])
        LtP = ring("LtP", 2, [128, 512])
        E = ring("E", 1, [128, 2, 512])
        Kt32 = ring("Kt32", 1, [128, 2, 512])
        Qpad = ring("Qpad", 1, [128, 2, 2, 512], BF16)
        KtT = ring("KtT", 1, [128, 2, 512], BF16)
        Kh_bf = ring("Kh_bf", 4, [128, 512], BF16)
        AT_bf = ring("AT_bf", 2, [128, 512], BF16)
        osq = ring("osq", 2, [128, 512], BF16)
        yo = ring("yo", 2, [128, 1024])
        stats = ring("stats", 2, [128, 12])
        mv = ring("mv", 2, [128, 2])
        lr = ring("lr", 2, [128, 2])
        Khm = ring("Khm", 2, [128, 4, 256], BF16)

        class View:
            def __init__(self, t, b):
                self.t = t
                self.b = b
        hres1 = View(stage_t[:, 0, 0:1024], stage_b[0])
        lnv1 = View(stage_t[:, 1, 0:512], stage_b[1])
        on1 = View(stage_t[:, 1, 512:1024], stage_b[1])

        st.enter_context(nc.Block())

        T.dma("sp", ident.t[:], ident_d, writes=[ident.b])
        T.op("pool", lambda: G.memset(faT[0].t[:], 0.0), writes=[faT[0].b])
        T.op("pool", lambda: G.memset(faT[0].t[32:64, :], 1.0), writes=[faT[0].b])
        T.op("pool", lambda: G.memset(Qpad[0].t[:], 0.0), writes=[Qpad[0].b])
        for i in range(2):
            T.op("pool", lambda i=i: G.memset(S_bf[i].t[:], 0.0), writes=[S_bf[i].b])
        T.op("pool", lambda: G.memset(S.t[:], 0.0), writes=[S.b])
        for i in range(4):
            T.op("pool", lambda i=i: G.memset(Kh_bf[i].t[:], 0.0), writes=[Kh_bf[i].b])
        T.op("pool", lambda: G.memset(ones_bf.t[:], 1.0), writes=[ones_bf.b])
        T.op("pool", lambda: G.memset(neghalf.t[:], -0.5), writes=[neghalf.b])
        sg_v = sgla_d.rearrange("s (b hp) k v -> hp k s b v", hp=2)

        def load_S0_group(grp, dst):
            dv = dst.t[:].rearrange("p (s b v) -> p s b v", s=4, b=2)
            for hp in range(2):
                for b in range(2):
                    T.dma("sp", dv[hp * 64:(hp + 1) * 64, :, b, :], sg_v[hp, :, grp * 4:(grp + 1) * 4, b, :], writes=[dst.b])
            return dv

        def emit_consts():
            T.dma("sp", tri.t[:], tri_d, writes=[tri.b])
            T.dma("sp", mask.t[:], mask_d, writes=[mask.b])
            T.dma("sp", ind.t[:], ind_d, writes=[ind.b])
            T.dma("act", pscale.t[:], pscale_d, writes=[pscale.b])
            T.dma("act", ng.t[:], ng_d, writes=[ng.b])
            T.dma("act", ln_g.t[:], ln_g_d.partition_broadcast(128), writes=[ln_g.b])
            T.dma("act", ln_b.t[:], ln_b_d.partition_broadcast(128), writes=[ln_b.b])
            def nstage():
                sv, sbuf_, own = stg[stg_i[0] % len(stg)]
                stg_i[0] += 1
                return sv, sbuf_, ([sbuf_] + ([own] if own is not None else []))
            sv, sb_, rd = nstage()
            wmv = sv[:, 0:512].rearrange("p (g e) -> p g e", g=4)
            T.dma("sp", wmv, w_map_d.rearrange("g c e -> c g e"), writes=[sb_])
            T.op("dve", lambda wmv=wmv: V.tensor_copy(out=w_map_bf.t[:], in_=wmv), reads=rd, writes=[w_map_bf.b])
            sv, sb_, rd = nstage()
            wav = sv[:, 0:256]
            T.op("pool", lambda wav=wav: G.memset(wav, 0.0), reads=rd[1:], writes=[sb_])
            T.dma("sp", sv[0:16, 0:256], w_a2_d, writes=[sb_])
            T.dma("sp", sv[32:33, 0:256], b_a_d, writes=[sb_])
            T.op("dve", lambda wav=wav: V.tensor_copy(out=w_a2_bf.t[:], in_=wav), reads=rd, writes=[w_a2_bf.b])
            sv, sb_, rd = nstage()
            T.op("pool", lambda sv=sv: G.memset(sv, 0.0), reads=rd[1:], writes=[sb_])
            for hh in range(2):
                T.dma("sp", sv[0:120, hh * 512:(hh + 1) * 512], spool_d[8 * hh:8 * hh + 8].rearrange("s j c -> (s j) c"), writes=[sb_])
            T.op("dve", lambda sv=sv: V.tensor_copy(out=bufS_bf.t[:], in_=sv.rearrange("p (h c) -> p h c", h=2)), reads=rd, writes=[bufS_bf.b])
            for g in range(4):
                sv, sb_, rd = nstage()
                bv = sv[:, 0:768].rearrange("p (k t) -> p k t", k=6)
                T.dma("sp", bv, bands_d[:, g, :, :], writes=[sb_])
                if g % 2 == 0:
                    T.op("dve", lambda g=g, bv=bv: V.tensor_copy(out=bands.t[:, g, :, :], in_=bv), reads=rd, writes=[bands.b])
                else:
                    T.op("dve", lambda g=g, bv=bv: V.tensor_copy(out=bands.t[:, g, :, :], in_=bv), reads=rd, writes=[bands.b])
            T.dma("act", pool_s_d[:, 0:7, :], spool_d[:, 8:15, :])

        col = {"u": 0, "gp": 512, "q": 1024, "k": 1280, "v": 1536, "gg": 2048, "fa": 2560}
        wg_b = {nm: [Buf("w_in_%s%d" % (nm, i)) for i in range(4)] for nm in ("qk", "fa", "u", "v", "gp", "gg")}
        wo_b = [Buf("w_out_%d" % k) for k in range(8)]
        stg = [(stage_t[:, 0, 0:1024], stage_b[0], None), (stage_t[:, 1, 0:1024], stage_b[1], None)]
        for own in (E[0], Kt32[0]):
            stg.append((own.t[:].rearrange("p a b -> p (a b)"), Buf(own.b.name + "_st"), own.b))
        for own in (yo[0], yo[1]):
            stg.append((own.t[:], Buf(own.b.name + "_st"), own.b))
        stg_i = [0]
        cast_eng = [0]

        def stream_cast(src_ap, shape3, dst_ap, dst_buf, eng):
            sv, sbuf_, own = stg[stg_i[0] % len(stg)]
            stg_i[0] += 1
            a_, b_ = shape3
            v = sv[:, 0:a_ * b_].rearrange("p (a b) -> p a b", a=a_)
            T.dma("sp", v, src_ap, writes=[sbuf_])
            rd = [sbuf_] + ([own] if own is not None else [])
            if eng == "pool":
                T.op("pool", lambda: G.tensor_copy(out=dst_ap, in_=v), reads=rd, writes=[dst_buf])
            else:
                T.op("dve", lambda: V.tensor_copy(out=dst_ap, in_=v), reads=rd, writes=[dst_buf])

        def load_x(ti):
            T.dma("sp", xa[ti % NX].t[:], x_d[ti * 128:(ti + 1) * 128, :], writes=[xa[ti % NX].b])

        xsched = []
        for j in range(4):
            c = [4 * 0 + j, 4 * 1 + j, 4 * 0 + j, 4 * 2 + j, 4 * 1 + j, 4 * 3 + j, 4 * 2 + j]
            if j == 0:
                c.append(16)
            c.append(4 * 3 + j)
            if j == 0:
                c.append(16)
            xsched.append(c)
        xpos = [0, 0, 0, 0]

        def x_advance(slot):
            cur = xsched[slot][xpos[slot]]
            xpos[slot] += 1
            if xpos[slot] < len(xsched[slot]):
                nxt = xsched[slot][xpos[slot]]
                if nxt != cur:
                    load_x(nxt)

        for ti in range(4):
            load_x(ti)
        w_in_v = w_in_d.rearrange("(k p) c -> p k c", p=128)
        groups = {"qk": (1024, 512), "fa": (2560, 16), "u": (0, 512), "v": (1536, 512), "gp": (512, 512), "gg": (2048, 512)}

        def stream_group(nm):
            c0, wd = groups[nm]
            if nm == "fa":
                stream_cast(w_in_v[:, :, c0:c0 + 16], (8, 16), w_in_bf.t[:, :, c0:c0 + 16], wg_b[nm][0], "dve")
                return
            for kk in range(4):
                cast_eng[0] += 1
                stream_cast(w_in_v[:, 2 * kk:2 * kk + 2, c0:c0 + wd], (2, wd), w_in_bf.t[:, 2 * kk:2 * kk + 2, c0:c0 + wd], wg_b[nm][kk],
                            "act" if cast_eng[0] % 2 == 0 else "dve")

        def stream_w_out():
            for kc in range(8):
                stream_cast(w_out_d[kc * 128:(kc + 1) * 128, :].rearrange("p (a b) -> p a b", a=1), (1, 1024),
                            w_out_bf.t[:, kc:kc + 1, :], wo_b[kc], "pool" if kc % 2 == 1 else "dve")
        wcol_b = {"u": wg_b["u"], "gp": wg_b["gp"], "q": wg_b["qk"], "k": wg_b["qk"], "v": wg_b["v"], "gg": wg_b["gg"], "fa": wg_b["fa"][0:1]}

        sts = [(0, 4), (4, 4), (8, 4), (12, 4), (16, 1)]
        xTb = xT[0]
        go_v = gla_s_d.rearrange("s (b hp) k v -> hp k s b v", hp=2)
        pj_i = [0]
        prologue_mode = [False]

        def pbank():
            bk = banks[pj_i[0] % 2]
            pj_i[0] += 1
            return bk
        ms_i = [0]

        def mbank():
            bk = banks[2 + ms_i[0] % 4]
            ms_i[0] += 1
            return bk
        BK_A, BK_O, BK_S, BK_Q = banks[2], banks[3], banks[4], banks[5]
        c1_early = [None]
        E_b = [Buf("E_b0"), Buf("E_b1")]
        BK_O2 = [banks[3], banks[5]]

        def unit_A(t0, j):
            ti = t0 + j
            xs = xa[ti % NX]

            def tr():
                ins = None
                for kc in range(8):
                    ins = PE.transpose(out=pbig[0].t[:, kc * 128:(kc + 1) * 128], in_=xs.t[:, kc * 128:(kc + 1) * 128], identity=ident.t[:])
                return ins
            T.op("pe", tr, reads=[xs.b, ident.b], writes=[pbig[0].b])
            T.op("act", lambda: A.copy(out=xTb.t[:, :, j * 128:(j + 1) * 128], in_=pbig[0].t[:].rearrange("p (k t) -> p k t", k=8)),
                 reads=[pbig[0].b], writes=[xTb.b])
            x_advance(ti % NX)

        def proj_fm(W, nm, off, m, evac):
            bk = pbank()
            c0 = col[nm] + off

            def f():
                ins = None
                for kc in range(8):
                    ins = PE.matmul(bk.t[0:m, 0:W], lhsT=w_in_bf.t[:, kc, c0:c0 + m], rhs=xTb.t[:, kc, 0:W], start=(kc == 0), stop=(kc == 7))
                return ins
            T.op("pe", f, reads=wcol_b[nm] + [xTb.b], writes=[bk.b])
            evac(bk)

        def unit_fa(W):
            proj_fm(W, "fa", 0, 16, lambda bk: T.op("act", lambda: A.copy(out=faT[0].t[0:16, 0:W], in_=bk.t[0:16, 0:W]), reads=[bk.b], writes=[faT[0].b]))

        def unit_q(W, b):
            proj_fm(W, "q", b * 128, 128, lambda bk: T.op("act", lambda: A.copy(out=qT[0].t[:, b, 0:W], in_=bk.t[:, 0:W]), reads=[bk.b], writes=[qT[0].b]))

        def unit_k(W, b):
            proj_fm(W, "k", b * 128, 128, lambda bk: T.op("act", lambda: A.copy(out=kT[0].t[:, b, 0:W], in_=bk.t[:, 0:W]), reads=[bk.b], writes=[kT[0].b]))

        def unit_gp(W):
            for g in range(4):
                proj_fm(W, "gp", g * 128, 128, lambda bk, g=g: T.op("act", lambda: A.activation(out=sgp[0].t[:, g, 0:W], in_=bk.t[:, 0:W], func=AF.Silu),
                                                                     reads=[bk.b], writes=[sgp[0].b]))

        def unit_gg(W):
            for h in range(4):
                proj_fm(W, "gg", h * 128, 128, lambda bk, h=h: T.op("act", lambda: A.activation(out=sgg[0].t[:, h, 0:W], in_=bk.t[:, 0:W], func=AF.Silu),
                                                                     reads=[bk.b], writes=[sgg[0].b]))
            for h in range(4):
                T.op("dve", lambda h=h: V.tensor_scalar(out=sgg[0].t[:, h, 0:W], in0=sgg[0].t[:, h, 0:W], scalar1=ng.t[:, h:h + 1], scalar2=None, op0=ALU.mult),
                     reads=[sgg[0].b, ng.b], writes=[sgg[0].b])

        def unit_tm(t0, j, nm, samp):
            ti = t0 + j
            bk = pbank()

            def f():
                ins = None
                for kc in range(8):
                    ins = PE.matmul(bk.t[:, 0:512], lhsT=xTb.t[:, kc, j * 128:(j + 1) * 128], rhs=w_in_bf.t[:, kc, col[nm]:col[nm] + 512],
                                    start=(kc == 0), stop=(kc == 7))
                return ins
            T.op("pe", f, reads=wcol_b[nm] + [xTb.b], writes=[bk.b])
            d = (u_bf[ti % 5] if nm == "u" else v_bf[ti % 4])
            T.op("act", lambda: A.copy(out=d.t[:], in_=bk.t[:, 0:512]), reads=[bk.b], writes=[d.b])
            if nm == "u" and (ti == 15 or samp):
                po = yo[0]
                T.op("act", lambda: A.copy(out=po.t[:, 0:512], in_=bk.t[:, 0:512]), reads=[bk.b], writes=[po.b])
                if ti == 15:
                    T.dma("sp", pool_p_d, po.t[113:128, 0:512], reads=[po.b])
                else:
                    for seq in range(16):
                        T.dma("sp", pool_s_d[seq, 7:15, :], po.t[seq * 8:(seq + 1) * 8, 0:512], reads=[po.b])

        def early_units(sti):
            t0, nt = sts[sti]
            W = 128 * nt
            samp = (sti == 4)
            us = [("A", lambda j=j: unit_A(t0, j)) for j in range(nt)]
            us.append(("fa", lambda: unit_fa(W)))
            us += [("q", lambda b=b: unit_q(W, b)) for b in range(2)]
            us += [("k", lambda b=b: unit_k(W, b)) for b in range(2)]
            us += [("u", lambda j=j: unit_tm(t0, j, "u", samp)) for j in range(nt)]
            us.append(("gp", lambda: unit_gp(W)))
            return us

        def late_units(sti):
            t0, nt = sts[sti]
            W = 128 * nt
            samp = (sti == 4)
            return [lambda j=j: unit_tm(t0, j, "v", samp) for j in range(nt)] + [lambda: unit_gg(W)]

        fillers = []
        late_fillers = []

        def lfill(n=1):
            for _ in range(n):
                if late_fillers:
                    late_fillers.pop(0)()

        def fill(n=1, kinds=None):
            for _ in range(n):
                if fillers and (kinds is None or fillers[0][0] in kinds):
                    fillers.pop(0)[1]()

        for nm_ in ("qk", "fa", "u"):
            stream_group(nm_)
        emit_consts()
        for nm_ in ("gp", "v", "gg"):
            stream_group(nm_)
        stream_w_out()
        prologue_mode[0] = True
        eu = early_units(0)
        lu = late_units(0)
        for k_, u_ in eu:
            u_()
        for u_ in lu:
            u_()
        prologue_mode[0] = False

        for sti, (t0, nt) in enumerate(sts):
            W = 128 * nt
            samp = (sti == 4)
            ci = 1 if samp else 0
            if sti + 1 < len(sts):
                fillers.extend(early_units(sti + 1))
            def C_part1(sti):
                t0, nt = sts[sti]
                W = 128 * nt
                samp = (sti == 4)
                cbk = []
                for g in range(4):
                    bk = banks[2 + g]
                    cbk.append(bk)

                    def f(bk=bk, g=g):
                        ins = None
                        for j in range(nt):
                            ti = t0 + j
                            ucur = u_bf[ti % 5]
                            o = bk.t[:, j * 128:(j + 1) * 128]
                            lw = ucur.t[:, g * 128:(g + 1) * 128]
                            if samp:
                                PE.matmul(o, lhsT=lw, rhs=bands.t[:, g, 3, :], start=True, stop=False)
                                PE.matmul(o, lhsT=bufS_bf.t[:, 0, g * 128:(g + 1) * 128], rhs=bands.t[:, g, 4, :], start=False, stop=False)
                                ins = PE.matmul(o, lhsT=bufS_bf.t[:, 1, g * 128:(g + 1) * 128], rhs=bands.t[:, g, 5, :], start=False, stop=True)
                            elif ti == 0:
                                ins = PE.matmul(o, lhsT=lw, rhs=bands.t[:, g, 2, :], start=True, stop=True)
                            else:
                                PE.matmul(o, lhsT=lw, rhs=bands.t[:, g, 0, :], start=True, stop=False)
                                ins = PE.matmul(o, lhsT=u_bf[(ti - 1) % 5].t[:, g * 128:(g + 1) * 128], rhs=bands.t[:, g, 1, :], start=False, stop=True)
                        return ins
                    urd = [bands.b] + [u_bf[(t0 + j) % 5].b for j in range(nt)]
                    if t0 > 0 and not samp:
                        urd.append(u_bf[(t0 - 1) % 5].b)
                    if samp:
                        urd.append(bufS_bf.b)
                    T.op("pe", f, reads=urd, writes=[bk.b])
                    T.op("act", lambda bk=bk, g=g: A.copy(out=pooledT[0].t[:, g, 0:W], in_=bk.t[:, 0:W]), reads=[bk.b], writes=[pooledT_g[g]])
                return cbk
            if c1_early[0] is None:
                cbk = C_part1(sti)
            else:
                cbk = c1_early[0]
                c1_early[0] = None
            lfill(2)
            for g in range(4):
                bk2 = cbk[g]
                T.op("pe", lambda bk2=bk2, g=g: PE.matmul(bk2.t[:, 0:W], lhsT=w_map_bf.t[:, g, :], rhs=pooledT[0].t[:, g, 0:W], start=True, stop=True),
                     reads=[w_map_bf.b, pooledT_g[g]], writes=[bk2.b])
                T.op("dve", lambda bk2=bk2, g=g: V.scalar_tensor_tensor(out=mixT[0].t[:, g, 0:W], in0=bk2.t[:, 0:W], scalar=pscale.t[:, g:g + 1],
                                                                        in1=sgp[0].t[:, g, 0:W], op0=ALU.mult, op1=ALU.mult),
                     reads=[bk2.b, pscale.b, sgp[0].b], writes=[mixT[0].b])
                if g % 2 == 1:
                    lfill(1)
            for jp in range(0, nt, 2):
                npair = min(2, nt - jp)
                wp = 256 * npair
                bk = mbank()

                def fg(bk=bk, jp=jp, npair=npair):
                    ins = None
                    for jj in range(npair):
                        ins = PE.matmul(bk.t[:, jj * 256:(jj + 1) * 256], lhsT=faT[0].t[:, (jp + jj) * 128:(jp + jj + 1) * 128], rhs=w_a2_bf.t[:],
                                        start=True, stop=True)
                    return ins
                T.op("pe", fg, reads=[faT[0].b, w_a2_bf.b], writes=[bk.b])
                lp = LtP[((t0 + jp) % 4) // 2]
                T.op("act", lambda bk=bk, wp=wp: A.activation(out=expg[0].t[:, 0:wp], in_=bk.t[:, 0:wp], func=AF.Exp, scale=-1.0),
                     reads=[bk.b], writes=[expg[0].b])
                T.op("act", lambda lp=lp, wp=wp: A.activation(out=lp.t[:, 0:wp], in_=expg[0].t[:, 0:wp], func=AF.Ln, bias=1.0),
                     reads=[expg[0].b], writes=[lp.b])
            lfill(len(late_fillers))
            fill(2, ("A",))
            for b in range(2):
                bk = mbank()

                def f(bk=bk, b=b):
                    ins = None
                    for j in range(nt):
                        ins = PE.matmul(bk.t[:, j * 128:(j + 1) * 128], lhsT=LtP[((t0 + j) % 4) // 2].t[:, ((t0 + j) % 2) * 256 + b * 128:((t0 + j) % 2) * 256 + (b + 1) * 128], rhs=tri.t[:, ci, :],
                                        start=True, stop=True)
                    return ins
                T.op("pe", f, reads=[tri.b] + [LtP[((t0 + j) % 4) // 2].b for j in range(nt)], writes=[bk.b])
                T.op("act", lambda bk=bk, b=b: A.activation(out=E[0].t[:, b, 0:W], in_=bk.t[:, 0:W], func=AF.Exp), reads=[bk.b], writes=[E[0].b, E_b[b]])
                T.op("act", lambda bk=bk, b=b: A.activation(out=Kt32[0].t[:, b, 0:W], in_=bk.t[:, 0:W], func=AF.Exp, scale=-1.0), reads=[bk.b], writes=[Kt32[0].b])
            for b in range(2):
                for hp in range(2):
                    ps_ = slice(hp * 64, (hp + 1) * 64)
                    T.op("dve", lambda b=b, hp=hp, ps_=ps_: V.scalar_tensor_tensor(out=Qpad[0].t[ps_, b, hp, 0:W], in0=qT[0].t[ps_, b, 0:W], scalar=0.125,
                                                                                  in1=E[0].t[ps_, b, 0:W], op0=ALU.mult, op1=ALU.mult),
                         reads=[qT[0].b, E_b[b]], writes=[Qpad[0].b])
            T.op("pool", lambda: G.tensor_tensor(out=KtT[0].t[:, :, 0:W], in0=kT[0].t[:, :, 0:W], in1=Kt32[0].t[:, :, 0:W], op=ALU.mult),
                 reads=[kT[0].b, Kt32[0].b], writes=[KtT[0].b])
            for b in range(2):
                if samp:
                    ebc = E[0].t[:, b, 7:128:8].unsqueeze(2).to_broadcast([128, 16, 8])
                    kv = Kt32[0].t[:, b, 0:128].rearrange("p (s t) -> p s t", t=8)
                else:
                    ebc = E[0].t[:, b, 127:512:128].unsqueeze(2).to_broadcast([128, 4, 128])
                    kv = Kt32[0].t[:, b, :].rearrange("p (s t) -> p s t", t=128)
                T.op("pool", lambda kv=kv, ebc=ebc: G.tensor_tensor(out=kv, in0=kv, in1=ebc, op=ALU.mult),
                     reads=[Kt32[0].b, E[0].b], writes=[Kt32[0].b])
            T.op("pool", lambda: G.tensor_tensor(out=Kt32[0].t[:, :, 0:W], in0=kT[0].t[:, :, 0:W], in1=Kt32[0].t[:, :, 0:W], op=ALU.mult),
                 reads=[kT[0].b, Kt32[0].b], writes=[Kt32[0].b])
            fill(2, ("A",))
            fill(1, ("fa",))
            fill(2, ("q",))
            for j in range(nt):
                ti = t0 + j
                bk = mbank()

                def f(bk=bk, j=j):
                    ins = None
                    for b in range(2):
                        ins = PE.transpose(out=bk.t[:, b * 128:(b + 1) * 128], in_=Kt32[0].t[:, b, j * 128:(j + 1) * 128], identity=ident.t[:])
                    return ins
                T.op("pe", f, reads=[Kt32[0].b, ident.b], writes=[bk.b])
                if samp:
                    T.op("act", lambda bk=bk, ti=ti: A.copy(out=Kh_bf[ti % 4].t[:, 0:256], in_=bk.t[:, 0:256]), reads=[bk.b], writes=[Kh_bf[ti % 4].b])
                else:
                    kdst = bass.AP(Kh_bf[ti % 4].t, 0, [[512, 128], [256, 2], [192, 2], [1, 64]])
                    ksrc = bass.AP(bk.t, 0, [[512, 128], [128, 2], [64, 2], [1, 64]])
                    T.op("act", lambda kdst=kdst, ksrc=ksrc: A.copy(out=kdst, in_=ksrc), reads=[bk.b], writes=[Kh_bf[ti % 4].b])
            lfill(len(late_fillers))
            def E1a(j):
                ti = t0 + j
                tsl = slice(j * 128, (j + 1) * 128)
                a2 = ti % 2

                def fA():
                    ins = None
                    for b in range(2):
                        ins = PE.matmul(BK_A.t[:, b * 256:(b + 1) * 256].rearrange("p (h t) -> p h t", h=2), lhsT=KtT[0].t[:, b, tsl],
                                        rhs=Qpad[0].t[:, b, :, tsl], start=True, stop=True)
                    return ins
                T.op("pe", fA, reads=[KtT[0].b, Qpad[0].b], writes=[BK_A.b])
                T.op("dve", lambda: V.tensor_tensor(out=AT_bf[a2].t[:].rearrange("p (h t) -> p h t", h=4),
                                                    in0=BK_A.t[:, 0:512].rearrange("p (h t) -> p h t", h=4),
                                                    in1=mask.t[:, 0, :].unsqueeze(1).to_broadcast([128, 4, 128]), op=ALU.mult),
                     reads=[BK_A.b, mask.b], writes=[AT_bf[a2].b])

            def E1b(j):
                ti = t0 + j
                tsl = slice(j * 128, (j + 1) * 128)
                a2 = ti % 2
                bkO = BK_O2[ti % 2]
                sprev = S_bf[ti % 2]
                snext = S_bf[(ti + 1) % 2]

                def fO():
                    ins = None
                    for b in range(2):
                        ob = bkO.t[:, b * 256:(b + 1) * 256].rearrange("p (h t) -> p h t", h=2)
                        PE.matmul(ob, lhsT=sprev.t[:, b, :], rhs=Qpad[0].t[:, b, :, tsl], start=(b == 0), stop=False, skip_group_check=True)
                    for h in range(4):
                        ins = PE.matmul(bkO.t[:, h * 128:(h + 1) * 128], lhsT=v_bf[ti % 4].t[:, h * 128:(h + 1) * 128],
                                        rhs=AT_bf[a2].t[:, h * 128:(h + 1) * 128], start=False, stop=True, skip_group_check=True)
                    return ins
                T.op("pe", fO, reads=[Qpad[0].b, AT_bf[a2].b, v_bf[ti % 4].b, sprev.b], writes=[bkO.b])

                def fS():
                    ins = None
                    for b in range(2):
                        for hp in range(2):
                            ins = PE.matmul(BK_S.t[:, b * 128:(b + 1) * 128], lhsT=Kh_bf[ti % 4].t[:, (2 * b + hp) * 128:(2 * b + hp + 1) * 128],
                                            rhs=v_bf[ti % 4].t[:, (2 * b + hp) * 128:(2 * b + hp + 1) * 128], start=(hp == 0), stop=(hp == 1))
                    return ins
                T.op("pe", fS, reads=[Kh_bf[ti % 4].b, v_bf[ti % 4].b], writes=[BK_S.b])
                for b in range(2):
                    T.op("dve", lambda b=b: V.scalar_tensor_tensor(
                        out=S.t[:, b, :], in0=S.t[:, b, :], scalar=E[0].t[:, b, j * 128 + 127:j * 128 + 128],
                        in1=BK_S.t[:, b * 128:(b + 1) * 128], op0=ALU.mult, op1=ALU.add),
                        reads=[S.b, E[0].b, BK_S.b], writes=[S.b])
                T.op("act", lambda: A.copy(out=snext.t[:], in_=S.t[:]), reads=[S.b], writes=[snext.b])
                if ti == 15:
                    for b in range(2):
                        for hp in range(2):
                            T.dma("sp", gla_p_d[2 * b + hp], S.t[hp * 64:(hp + 1) * 64, b, :], reads=[S.b])

            def E2a(j):
                ti = t0 + j
                a2 = ti % 2
                bkO = BK_O2[ti % 2]
                T.op("act", lambda: A.activation(out=osq[a2].t[:], in_=bkO.t[:, 0:512], func=AF.Square), reads=[bkO.b], writes=[osq[a2].b])

            def E2b_pe(j):
                ti = t0 + j
                a2 = ti % 2
                bkQ = BK_A
                T.op("pe", lambda: PE.matmul(bkQ.t[:, 0:512], lhsT=ones_bf.t[:], rhs=osq[a2].t[:], start=True, stop=True),
                     reads=[ones_bf.b, osq[a2].b], writes=[bkQ.b])
                T.op("act", lambda: A.activation(out=lnv1.t, in_=bkQ.t[:, 0:512], func=AF.Ln, scale=1.0 / 128.0, bias=RMS_EPS),
                     reads=[bkQ.b], writes=[lnv1.b])

            def E2b(j):
                ti = t0 + j
                tsl = slice(j * 128, (j + 1) * 128)
                a2 = ti % 2
                bkO = BK_O2[ti % 2]
                bkQ = BK_A
                T.op("act", lambda: A.activation(out=lnv1.t, in_=lnv1.t, func=AF.Exp, scale=-0.5), reads=[lnv1.b], writes=[lnv1.b])
                T.op("dve", lambda: V.tensor_tensor(out=on1.t, in0=bkO.t[:, 0:512], in1=lnv1.t, op=ALU.mult),
                     reads=[bkO.b, lnv1.b], writes=[on1.b])
                T.op("pool", lambda: G.tensor_tensor(out=mixT[0].t[:, 4:8, tsl], in0=on1.t.rearrange("p (h t) -> p h t", h=4),
                                                      in1=sgg[0].t[:, :, tsl], op=ALU.mult),
                     reads=[on1.b, sgg[0].b], writes=[mixTg4[j]])

            def Fst_pe(j):
                tsl = slice(j * 128, (j + 1) * 128)

                def fY():
                    ins = None
                    for n in range(2):
                        for e in range(8):
                            ins = PE.matmul(pbig[0].t[:, n * 512:(n + 1) * 512], lhsT=mixT[0].t[:, e, tsl], rhs=w_out_bf.t[:, e, n * 512:(n + 1) * 512],
                                            start=(e == 0), stop=(e == 7))
                    return ins
                T.op("pe", fY, reads=[mixT[0].b, mixTg4[j]] + wo_b, writes=[pbig[0].b])

            def Fst(j):
                ti = t0 + j
                a2 = ti % 2
                xrs = xa[ti % NX]
                T.op("dve", lambda: V.scalar_tensor_tensor(out=hres1.t, in0=xrs.t[:], scalar=ALPHA, in1=pbig[0].t[:], op0=ALU.mult, op1=ALU.add),
                     reads=[xrs.b, pbig[0].b], writes=[hres1.b])
                x_advance(ti % NX)
                T.op("dve", lambda: V.bn_stats(out=stats[a2].t[:, 0:6], in_=hres1.t[:, 0:512]), reads=[hres1.b], writes=[stats[a2].b])
                T.op("dve", lambda: V.bn_stats(out=stats[a2].t[:, 6:12], in_=hres1.t[:, 512:1024]), reads=[hres1.b], writes=[stats[a2].b])
                T.op("dve", lambda: V.bn_aggr(out=mv[a2].t[:], in_=stats[a2].t[:]), reads=[stats[a2].b], writes=[mv[a2].b])
                T.op("pool", lambda: G.tensor_scalar(out=lr[a2].t[:, 0:1], in0=mv[a2].t[:, 1:2], scalar1=LN_EPS, scalar2=None, op0=ALU.add),
                     reads=[mv[a2].b], writes=[lr[a2].b])
                T.op("pool", lambda: G.tensor_tensor(out=lr[a2].t[:, 0:1], in0=lr[a2].t[:, 0:1], in1=neghalf.t[:, 0:1], op=ALU.pow),
                     reads=[lr[a2].b, neghalf.b], writes=[lr[a2].b])
                T.op("dve", lambda: V.tensor_scalar(out=yo[a2].t[:], in0=hres1.t, scalar1=mv[a2].t[:, 0:1], scalar2=lr[a2].t[:, 0:1],
                                                    op0=ALU.subtract, op1=ALU.mult),
                     reads=[hres1.b, mv[a2].b, lr[a2].b], writes=[yo[a2].b])
                T.op("dve", lambda: V.tensor_tensor(out=yo[a2].t[:], in0=yo[a2].t[:], in1=ln_g.t[:], op=ALU.mult), reads=[yo[a2].b, ln_g.b], writes=[yo[a2].b])
                T.op("pool", lambda: G.tensor_tensor(out=yo[a2].t[:], in0=yo[a2].t[:], in1=ln_b.t[:], op=ALU.add), reads=[yo[a2].b, ln_b.b], writes=[yo[a2].b])
                T.dma("sp", y_d[ti * 128:(ti + 1) * 128, :], yo[a2].t[:], reads=[yo[a2].b])

            if not samp:
                E1a(0)
                fill(1)
                E1b(0)
                E2a(0)
                for j in range(nt):
                    if j >= 1:
                        Fst_pe(j - 1)
                    E2b_pe(j)
                    if j + 1 < nt:
                        E1a(j + 1)
                    E2b(j)
                    if j in (0, 1, 2):
                        fill(1)
                    if j + 1 < nt:
                        E1b(j + 1)
                        E2a(j + 1)
                    if j in (0, 1, 3):
                        fill(1)
                    if j >= 1:
                        Fst(j - 1)
                if sti + 1 < len(sts):
                    while fillers:
                        fillers.pop(0)[1]()
                    c1_early[0] = C_part1(sti + 1)
                Fst_pe(nt - 1)
                Fst(nt - 1)
            if samp:
                ti = 16
                a2 = 0
                tsl = slice(0, 128)
                S0blk = [S0b.t[:], sgp[0].t[:].rearrange("p g (s v) -> p (g s) v", v=128)]
                S0blk_b = [S0b_h[0], sgp[0].b]
                for b in range(2):
                    T.dma_batch("sp", [(S0blk[b][hp * 64:(hp + 1) * 64, :, :], sg_v[hp, :, :, b, :]) for hp in range(2)], writes=[S0blk_b[b]])
                gbuf = [yo[0], yo[1], xa[1], xa[2]]
                gdv = [g_.t[:].rearrange("p (s b v) -> p s b v", s=4, b=2) for g_ in gbuf]
                for grp in range(4):
                    T.dma_batch("sp", [(gdv[grp][hp * 64:(hp + 1) * 64, :, b, :], sg_v[hp, :, grp * 4:(grp + 1) * 4, b, :])
                                       for hp in range(2) for b in range(2)], writes=[gbuf[grp].b])
                xTv = xT[0].t[:].rearrange("p k t -> p (k t)")
                S0bf = [xTv[:, b * 2048:(b + 1) * 2048].rearrange("p (s v) -> p s v", v=128) for b in range(2)]
                for b in range(2):
                    T.op("act", lambda b=b: A.copy(out=S0bf[b], in_=S0blk[b]), reads=[S0blk_b[b]], writes=[xT[0].b])
                kh = Kh_bf[16 % 4]
                qbs = [[Buf("sq_%d_%d" % (g_, q_)) for q_ in range(4)] for g_ in range(4)]
                for g_ in range(4):
                    ebc = bass.AP(E[0].t, g_ * 32 + 7, [[1024, 128], [8, 4], [512, 2], [0, 128]])
                    T.op("pool", lambda dv=gdv[g_], ebc=ebc: G.tensor_tensor(out=dv, in0=dv, in1=ebc, op=ALU.mult),
                         reads=[gbuf[g_].b, E[0].b], writes=[gbuf[g_].b] + qbs[g_])

                def supd(grp):
                    km = Khm[grp % 2]
                    T.op("dve", lambda km=km, grp=grp: V.tensor_tensor(out=km.t[:], in0=kh.t[:, 0:256].unsqueeze(1).to_broadcast([128, 4, 256]),
                                                                       in1=ind.t[:, grp * 4:(grp + 1) * 4].unsqueeze(2).to_broadcast([128, 4, 256]), op=ALU.mult),
                         reads=[kh.b, ind.b], writes=[km.b])
                    sdst = gbuf[grp]
                    dv = gdv[grp]
                    qb = qbs[grp]
                    for qq in range(4):
                        bk = mbank()

                        def fS(bk=bk, qq=qq, km=km):
                            ins = None
                            for b in range(2):
                                ins = PE.matmul(bk.t[:, b * 256:(b + 1) * 256], lhsT=km.t[:, qq, b * 128:(b + 1) * 128],
                                                rhs=v_bf[16 % 4].t[:, b * 256:(b + 1) * 256], start=True, stop=True)
                            return ins
                        T.op("pe", fS, reads=[km.b, v_bf[16 % 4].b], writes=[bk.b])
                        bv = bk.t[:, 0:512].rearrange("p (b h v) -> p b h v", b=2, h=2)
                        for hp in range(2):
                            ps_ = slice(hp * 64, (hp + 1) * 64)
                            T.op("dve", lambda bv=bv, qq=qq, hp=hp, ps_=ps_, dv=dv: V.tensor_tensor(
                                out=dv[ps_, qq, :, :], in0=dv[ps_, qq, :, :], in1=bv[ps_, :, hp, :], op=ALU.add),
                                reads=[bk.b], writes=[qb[qq]] if hp == 0 else [qb[qq]])
                    T.dma_batch("sp", [(go_v[hp, :, grp * 4:(grp + 1) * 4, b, :], dv[hp * 64:(hp + 1) * 64, :, b, :])
                                       for hp in range(2) for b in range(2)], reads=qb + [sdst.b])

                def fA():
                    ins = None
                    for b in range(2):
                        ins = PE.matmul(BK_A.t[:, b * 256:(b + 1) * 256].rearrange("p (h t) -> p h t", h=2), lhsT=KtT[0].t[:, b, tsl],
                                        rhs=Qpad[0].t[:, b, :, tsl], start=True, stop=True)
                    return ins
                T.op("pe", fA, reads=[KtT[0].b, Qpad[0].b], writes=[BK_A.b])
                T.op("dve", lambda: V.tensor_tensor(out=AT_bf[a2].t[:].rearrange("p (h t) -> p h t", h=4),
                                                    in0=BK_A.t[:, 0:512].rearrange("p (h t) -> p h t", h=4),
                                                    in1=mask.t[:, 1, :].unsqueeze(1).to_broadcast([128, 4, 128]), op=ALU.mult),
                     reads=[BK_A.b, mask.b], writes=[AT_bf[a2].b])
                supd(0)
                bkO = BK_O

                def fO():
                    ins = None
                    first = True
                    for b in range(2):
                        for seq in range(16):
                            for hp in range(2):
                                c0 = b * 256 + hp * 128 + seq * 8
                                PE.matmul(bkO.t[:, c0:c0 + 8], lhsT=S0bf[b][:, seq, :], rhs=Qpad[0].t[:, b, hp, seq * 8:(seq + 1) * 8],
                                          start=first, stop=False, skip_group_check=True)
                                first = False
                    for h in range(4):
                        ins = PE.matmul(bkO.t[:, h * 128:(h + 1) * 128], lhsT=v_bf[16 % 4].t[:, h * 128:(h + 1) * 128],
                                        rhs=AT_bf[a2].t[:, h * 128:(h + 1) * 128], start=False, stop=True, skip_group_check=True)
                    return ins
                T.op("pe", fO, reads=[Qpad[0].b, xT[0].b, AT_bf[a2].b, v_bf[16 % 4].b], writes=[bkO.b])
                T.op("act", lambda: A.activation(out=osq[a2].t[:], in_=bkO.t[:, 0:512], func=AF.Square), reads=[bkO.b], writes=[osq[a2].b])
                bkQ = BK_Q
                T.op("pe", lambda: PE.matmul(bkQ.t[:, 0:512], lhsT=ones_bf.t[:], rhs=osq[a2].t[:], start=True, stop=True),
                     reads=[ones_bf.b, osq[a2].b], writes=[bkQ.b])
                T.op("act", lambda: A.activation(out=lnv1.t, in_=bkQ.t[:, 0:512], func=AF.Ln, scale=1.0 / 128.0, bias=RMS_EPS),
                     reads=[bkQ.b], writes=[lnv1.b])
                T.op("act", lambda: A.activation(out=lnv1.t, in_=lnv1.t, func=AF.Exp, scale=-0.5), reads=[lnv1.b], writes=[lnv1.b])
                T.op("dve", lambda: V.tensor_tensor(out=on1.t, in0=bkO.t[:, 0:512], in1=lnv1.t, op=ALU.mult),
                     reads=[bkO.b, lnv1.b], writes=[on1.b])
                T.op("dve", lambda: V.tensor_tensor(out=mixT[0].t[:, 4:8, tsl], in0=on1.t.rearrange("p (h t) -> p h t", h=4),
                                                    in1=sgg[0].t[:, :, tsl], op=ALU.mult),
                     reads=[on1.b, sgg[0].b], writes=[mixTg[0], mixTg4[0]])
                supd(1)
                xrs = xa[0]
                yb = xa[3]

                def fY():
                    ins = None
                    for n in range(2):
                        for e in range(8):
                            ins = PE.matmul(pbig[0].t[:, n * 512:(n + 1) * 512], lhsT=mixT[0].t[:, e, tsl], rhs=w_out_bf.t[:, e, n * 512:(n + 1) * 512],
                                            start=(e == 0), stop=(e == 7))
                    return ins
                T.op("pe", fY, reads=[mixT[0].b, mixTg[0]] + wo_b, writes=[pbig[0].b])
                supd(2)
                T.op("dve", lambda: V.scalar_tensor_tensor(out=hres1.t, in0=xrs.t[:], scalar=ALPHA, in1=pbig[0].t[:], op0=ALU.mult, op1=ALU.add),
                     reads=[xrs.b, pbig[0].b], writes=[hres1.b])
                T.op("dve", lambda: V.bn_stats(out=stats[a2].t[:, 0:6], in_=hres1.t[:, 0:512]), reads=[hres1.b], writes=[stats[a2].b])
                T.op("dve", lambda: V.bn_stats(out=stats[a2].t[:, 6:12], in_=hres1.t[:, 512:1024]), reads=[hres1.b], writes=[stats[a2].b])
                T.op("dve", lambda: V.bn_aggr(out=mv[a2].t[:], in_=stats[a2].t[:]), reads=[stats[a2].b], writes=[mv[a2].b])
                T.op("pool", lambda: G.tensor_scalar(out=lr[a2].t[:, 0:1], in0=mv[a2].t[:, 1:2], scalar1=LN_EPS, scalar2=None, op0=ALU.add),
                     reads=[mv[a2].b], writes=[lr[a2].b])
                T.op("pool", lambda: G.tensor_tensor(out=lr[a2].t[:, 0:1], in0=lr[a2].t[:, 0:1], in1=neghalf.t[:, 0:1], op=ALU.pow),
                     reads=[lr[a2].b, neghalf.b], writes=[lr[a2].b])
                T.op("dve", lambda: V.scalar_tensor_tensor(out=lr[a2].t[:, 1:2], in0=mv[a2].t[:, 0:1], scalar=-1.0, in1=lr[a2].t[:, 0:1],
                                                           op0=ALU.mult, op1=ALU.mult),
                     reads=[mv[a2].b, lr[a2].b], writes=[lr[a2].b])
                T.op("dve", lambda: V.tensor_scalar(out=yb.t[:], in0=hres1.t, scalar1=lr[a2].t[:, 0:1], scalar2=lr[a2].t[:, 1:2],
                                                    op0=ALU.mult, op1=ALU.add),
                     reads=[hres1.b, lr[a2].b], writes=[yb.b])
                T.op("pool", lambda: G.tensor_tensor(out=yb.t[:], in0=yb.t[:], in1=ln_g.t[:], op=ALU.mult), reads=[yb.b, ln_g.b], writes=[yb.b])
                T.op("pool", lambda: G.tensor_tensor(out=yb.t[:], in0=yb.t[:], in1=ln_b.t[:], op=ALU.add), reads=[yb.b, ln_b.b], writes=[yb.b])
                T.dma("sp", y_d[ti * 128:(ti + 1) * 128, :], yb.t[:], reads=[yb.b])
                supd(3)
            while fillers:
                fillers.pop(0)[1]()
            if sti + 1 < len(sts):
                late_fillers.extend(late_units(sti + 1))
        T.wait_all_dma("sp")
    return nc


_CACHE = {}


def kernel(x_prompt, x_sample, state_pool, state_gla, w_in, w_pool_map, pool_scale, w_gla_a2,
           b_gla_a, gla_norm_g, w_out, ln_g, ln_b):
    f = lambda a: np.ascontiguousarray(np.asarray(a, dtype=np.float32))
    x_prompt, x_sample, state_pool, state_gla = f(x_prompt), f(x_sample), f(state_pool), f(state_gla)
    if "nc" not in _CACHE:
        _CACHE["nc"] = build_program()
        _CACHE["consts"] = build_consts()
    nc = _CACHE["nc"]
    C = _CACHE["consts"]
    shared = {
        "w_in": f(w_in)[0], "w_map": f(w_pool_map)[0], "pscale": np.ascontiguousarray(f(pool_scale)[0].reshape(4, 128).T), "w_a2": f(w_gla_a2)[0],
        "b_a": f(b_gla_a)[0].reshape(1, 256), "ng": np.ascontiguousarray(f(gla_norm_g)[0].reshape(4, 128).T), "w_out": f(w_out)[0],
        "ln_g": f(ln_g)[0].reshape(1, 1024), "ln_b": f(ln_b)[0].reshape(1, 1024),
        "ident": C["ident"], "tri": C["tri"], "mask": C["mask"], "bands": C["bands"], "ind": C["ind"],
    }
    in_maps = []
    for c in range(NCORES):
        m = dict(shared)
        m["x"] = np.ascontiguousarray(np.concatenate([x_prompt[c], x_sample[16 * c:16 * c + 16].reshape(128, 1024)], axis=0))
        m["spool"] = np.ascontiguousarray(state_pool[0, 16 * c:16 * c + 16])
        m["sgla"] = np.ascontiguousarray(state_gla[0, 16 * c:16 * c + 16])
        in_maps.append(m)
    res = run_bass_kernel_spmd(nc, in_maps, core_ids=list(range(NCORES)))
    R = res.results
    y_prompt = np.stack([R[c]["y"][:2048] for c in range(NCORES)], 0).astype(np.float32)
    y_sample = np.concatenate([R[c]["y"][2048:].reshape(16, 8, 1024) for c in range(NCORES)], 0).astype(np.float32)
    pool_p = np.stack([R[c]["pool_p"] for c in range(NCORES)], 0)[None].astype(np.float32)
    gla_p = np.stack([R[c]["gla_p"] for c in range(NCORES)], 0)[None].astype(np.float32)
    pool_s = np.concatenate([R[c]["pool_s"] for c in range(NCORES)], 0)[None].astype(np.float32)
    gla_s = np.concatenate([R[c]["gla_s"] for c in range(NCORES)], 0)[None].astype(np.float32)
    return (y_prompt, y_sample, pool_p, gla_p, pool_s, gla_s)
```

```python
from contextlib import ExitStack

import numpy as np
import concourse.bass as bass
import concourse.mybir as mybir
from concourse.bass_utils import run_bass_kernel_spmd

F32 = mybir.dt.float32
BF16 = mybir.dt.bfloat16
AF = mybir.ActivationFunctionType
ALU = mybir.AluOpType

NCORES = 8
NTOK = 2176
NTILE = 17
WINS = (2, 4, 8, 16)
ALPHA = 2.0 ** 0.25
LN_EPS = 1e-5
RMS_EPS = 1e-6


def build_consts():
    c = {}
    c["ident"] = np.eye(128, dtype=np.float32)
    s = np.arange(128)[:, None]
    t = np.arange(128)[None, :]
    tri_p = np.where(s <= t, -1.0 / 16.0, 0.0).astype(np.float32)
    same = (s // 8) == (t // 8)
    tri_s = np.where(same & (s <= t), -1.0 / 16.0, 0.0).astype(np.float32)
    c["tri"] = np.ascontiguousarray(np.stack([tri_p, tri_s], axis=1))
    mask_p = (s <= t).astype(np.float32)
    mask_s = (same & (s <= t)).astype(np.float32)
    c["mask"] = np.ascontiguousarray(np.stack([mask_p, mask_s], axis=1))
    bands = np.zeros((128, 4, 6, 128), np.float32)
    for g, w in enumerate(WINS):
        bands[:, g, 0, :] = np.where((s <= t) & (s >= t - w + 1), 1.0 / w, 0.0) - (s == t)
        bands[:, g, 1, :] = np.where((s - 128) >= (t - w + 1), 1.0 / w, 0.0)
        cnt = np.minimum(t + 1, w)
        bands[:, g, 2, :] = np.where((s <= t) & (s >= t - w + 1), 1.0 / cnt, 0.0) - (s == t)
        ts = t % 8
        ss = s % 8
        bands[:, g, 3, :] = np.where(same & (ss <= ts) & (ss >= ts - w + 1), 1.0 / w, 0.0) - (s == t)
        for hh in range(2):
            m = np.zeros((128, 128), np.float32)
            for r in range(120):
                sl, j = divmod(r, 15)
                seq = 8 * hh + sl
                for tt in range(8):
                    if j >= 15 + tt - w + 1:
                        m[r, seq * 8 + tt] = 1.0 / w
            bands[:, g, 4 + hh, :] = m
    c["bands"] = bands
    c["ind"] = ((np.arange(128)[:, None] // 8) == np.arange(16)[None, :]).astype(np.float32)
    return c


class Buf:
    __slots__ = ("name", "w", "r", "psum")

    def __init__(self, name, psum=False):
        self.name = name
        self.w = None
        self.r = {}
        self.psum = psum


class Eng:
    def __init__(self, name, h, sem):
        self.name = name
        self.h = h
        self.sem = sem
        self.cnt = 0
        self.seen = {}
        self.dsems = []
        self.dnext = 0


class Tracker:
    NDSEM = 10

    def __init__(self, nc, stack, dma_queues=("sp", "act", "pool")):
        self.nc = nc
        self.eng = {}
        for name, h in (("pe", nc.tensor), ("act", nc.scalar), ("dve", nc.vector),
                        ("pool", nc.gpsimd), ("sp", nc.sync)):
            sem = stack.enter_context(nc.semaphore("s_" + name))
            self.eng[name] = Eng(name, h, sem)
        for q in dma_queues:
            e = self.eng[q]
            for i in range(self.NDSEM):
                s = stack.enter_context(nc.semaphore("d_%s%d" % (q, i)))
                e.dsems.append([s, 0])

    def _need(self, e, dep):
        if dep is None:
            return
        if dep[0] == "m":
            for d in dep[1]:
                self._need(e, d)
            return
        if dep[0] == "d":
            _, sem, val = dep
            key = ("d", sem.num)
            if e.seen.get(key, 0) >= val:
                return
            e.h.wait_ge(sem, val)
            e.seen[key] = val
            return
        _, oname, cnt = dep
        if e.seen.get(oname, 0) >= cnt:
            return
        e.h.wait_ge(self.eng[oname].sem, cnt)
        e.seen[oname] = cnt

    def op(self, ename, fn, reads=(), writes=()):
        e = self.eng[ename]
        for b in reads:
            self._need(e, b.w)
            if b.psum:
                for k, d in b.r.items():
                    if not (d[0] == "e" and d[1] == ename):
                        self._need(e, d)
        for b in writes:
            self._need(e, b.w)
            for k, d in b.r.items():
                self._need(e, d)
        ins = fn()
        ins.then_inc(e.sem, 1)
        e.cnt += 1
        dep = ("e", ename, e.cnt)
        for b in reads:
            b.r[ename] = dep
        for b in writes:
            b.w = dep
            b.r = {}
        return ins

    def dma(self, ename, out, in_, reads=(), writes=()):
        e = self.eng[ename]
        for b in reads:
            self._need(e, b.w)
        for b in writes:
            self._need(e, b.w)
            for k, d in b.r.items():
                self._need(e, d)
        slot = e.dsems[e.dnext % len(e.dsems)]
        e.dnext += 1
        sem, val = slot
        if val > 0:
            self._need(e, ("d", sem, val))
        e.h.dma_start(out=out, in_=in_).then_inc(sem, 16)
        slot[1] = val + 16
        dep = ("d", sem, val + 16)
        for b in writes:
            b.w = dep
            b.r = {}
        for b in reads:
            b.r[("d", sem.num)] = dep
        return dep

    def dma_batch(self, ename, pairs, reads=(), writes=()):
        e = self.eng[ename]
        for b in reads:
            self._need(e, b.w)
        for b in writes:
            self._need(e, b.w)
            for k, d in b.r.items():
                self._need(e, d)
        deps = []
        for out, in_ in pairs:
            slot = e.dsems[e.dnext % len(e.dsems)]
            e.dnext += 1
            sem, val = slot
            if val > 0:
                self._need(e, ("d", sem, val))
            e.h.dma_start(out=out, in_=in_).then_inc(sem, 16)
            slot[1] = val + 16
            deps.append(("d", sem, val + 16))
        dep = ("m", tuple(deps))
        for b in writes:
            b.w = dep
            b.r = {}
        for b in reads:
            b.r[("m", deps[0][1].num, deps[0][2])] = dep
        return dep

    def wait_all_dma(self, ename):
        e = self.eng[ename]
        for q in self.eng.values():
            for sem, val in q.dsems:
                if val > 0:
                    self._need(e, ("d", sem, val))


class TB:
    __slots__ = ("t", "b")

    def __init__(self, t, name, psum=False):
        self.t = t
        self.b = Buf(name, psum)


def build_program():
    nc = bass.Bass("TRN2", target_bir_lowering=False)
    dt_in = lambda name, shape: nc.dram_tensor(name, shape, F32, kind="ExternalInput").ap()
    dt_out = lambda name, shape: nc.dram_tensor(name, shape, F32, kind="ExternalOutput").ap()
    x_d = dt_in("x", [NTOK, 1024])
    spool_d = dt_in("spool", [16, 15, 512])
    sgla_d = dt_in("sgla", [16, 4, 64, 128])
    w_in_d = dt_in("w_in", [1024, 2576])
    w_map_d = dt_in("w_map", [4, 128, 128])
    pscale_d = dt_in("pscale", [128, 4])
    w_a2_d = dt_in("w_a2", [16, 256])
    b_a_d = dt_in("b_a", [1, 256])
    ng_d = dt_in("ng", [128, 4])
    w_out_d = dt_in("w_out", [1024, 1024])
    ln_g_d = dt_in("ln_g", [1, 1024])
    ln_b_d = dt_in("ln_b", [1, 1024])
    ident_d = dt_in("ident", [128, 128])
    tri_d = dt_in("tri", [128, 2, 128])
    mask_d = dt_in("mask", [128, 2, 128])
    bands_d = dt_in("bands", [128, 4, 6, 128])
    ind_d = dt_in("ind", [128, 16])
    y_d = dt_out("y", [NTOK, 1024])
    pool_p_d = dt_out("pool_p", [15, 512])
    gla_p_d = dt_out("gla_p", [4, 64, 128])
    pool_s_d = dt_out("pool_s", [16, 15, 512])
    gla_s_d = dt_out("gla_s", [16, 4, 64, 128])

    with ExitStack() as st:
        T = Tracker(nc, st)
        V = nc.vector
        A = nc.scalar
        G = nc.gpsimd
        PE = nc.tensor

        def sb(name, shape, dt=F32):
            return TB(st.enter_context(nc.sbuf_tensor(name, shape, dt)), name)

        def ring(name, n, shape, dt=F32):
            return [sb("%s%d" % (name, i), shape, dt) for i in range(n)]

        pbig = [TB(st.enter_context(nc.psum_tensor("pbig", [128, 1024], F32)), "pbig", True)]
        banks = [TB(st.enter_context(nc.psum_tensor("bank%d" % i, [128, 512], F32)), "bank%d" % i, True) for i in range(6)]
        bank_i = [0]

        def nbank():
            b = banks[bank_i[0] % len(banks)]
            bank_i[0] += 1
            return b

        w_in_bf = sb("w_in_bf", [128, 8, 2576], BF16)
        w_out_bf = sb("w_out_bf", [128, 8, 1024], BF16)
        stage_t = st.enter_context(nc.sbuf_tensor("stage", [128, 2, 1024], F32))
        stage_b = [Buf("stage0"), Buf("stage1")]
        w_map_bf = sb("w_map_bf", [128, 4, 128], BF16)
        w_a2_bf = sb("w_a2_bf", [128, 256], BF16)
        pscale = sb("pscale_sb", [128, 4])
        ng = sb("ng_sb", [128, 4])
        ln_g = sb("ln_g_sb", [128, 1024])
        ln_b = sb("ln_b_sb", [128, 1024])
        ident = sb("ident_sb", [128, 128])
        tri = sb("tri_sb", [128, 2, 128])
        mask = sb("mask_sb", [128, 2, 128])
        bands = sb("bands_bf", [128, 4, 6, 128], BF16)
        ind = sb("ind_sb", [128, 16])
        ones_bf = sb("ones_bf", [128, 128], BF16)
        neghalf = sb("neghalf", [128, 2])
        bufS_bf = sb("bufS_bf", [128, 2, 512], BF16)
        S0b = sb("S0b", [128, 16, 128])
        S0b_h = [Buf("S0b_h0"), Buf("S0b_h1")]
        S = sb("S", [128, 2, 128])
        S_bf = ring("S_bf", 2, [128, 2, 128], BF16)

        NX = 4
        xa = ring("xa", NX, [128, 1024])
        xT = ring("xT", 1, [128, 8, 512], BF16)
        faT = ring("faT", 1, [128, 512], BF16)
        qT = ring("qT", 1, [128, 2, 512])
        kT = ring("kT", 1, [128, 2, 512])
        sgp = ring("sgp", 1, [128, 4, 512])
        sgg = ring("sgg", 1, [128, 4, 512])
        u_bf = ring("u_bf", 5, [128, 512], BF16)
        v_bf = ring("v_bf", 4, [128, 512], BF16)
        pooledT = ring("pooledT", 1, [128, 4, 512], BF16)
        mixT = ring("mixT", 1, [128, 8, 512], BF16)
        mixTg = [Buf("mixTg0")]
        mixTg4 = [Buf("mixTg4_%d" % i) for i in range(4)]
        pooledT_g = [Buf("pooledT_g%d" % i) for i in range(4)]
        expg = ring("expg", 1, [128, 512])
        LtP = ring("LtP", 2, [128, 512])
        E = ring("E", 1, [128, 2, 512])
        Kt32 = ring("Kt32", 1, [128, 2, 512])
        Qpad = ring("Qpad", 1, [128, 2, 2, 512], BF16)
        KtT = ring("KtT", 1, [128, 2, 512], BF16)
        Kh_bf = ring("Kh_bf", 4, [128, 512], BF16)
        AT_bf = ring("AT_bf", 2, [128, 512], BF16)
        osq = ring("osq", 2, [128, 512], BF16)
        yo = ring("yo", 2, [128, 1024])
        stats = ring("stats", 2, [128, 12])
        mv = ring("mv", 2, [128, 2])
        lr = ring("lr", 2, [128, 2])
        Khm = ring("Khm", 2, [128, 4, 256], BF16)

        class View:
            def __init__(self, t, b):
                self.t = t
                self.b = b
        hres1 = View(stage_t[:, 0, 0:1024], stage_b[0])
        lnv1 = View(stage_t[:, 1, 0:512], stage_b[1])
        on1 = View(stage_t[:, 1, 512:1024], stage_b[1])

        st.enter_context(nc.Block())

        T.dma("sp", ident.t[:], ident_d, writes=[ident.b])
        T.op("pool", lambda: G.memset(faT[0].t[:], 0.0), writes=[faT[0].b])
        T.op("pool", lambda: G.memset(faT[0].t[32:64, :], 1.0), writes=[faT[0].b])
        T.op("pool", lambda: G.memset(Qpad[0].t[:], 0.0), writes=[Qpad[0].b])
        for i in range(2):
            T.op("pool", lambda i=i: G.memset(S_bf[i].t[:], 0.0), writes=[S_bf[i].b])
        T.op("pool", lambda: G.memset(S.t[:], 0.0), writes=[S.b])
        for i in range(4):
            T.op("pool", lambda i=i: G.memset(Kh_bf[i].t[:], 0.0), writes=[Kh_bf[i].b])
        T.op("pool", lambda: G.memset(ones_bf.t[:], 1.0), writes=[ones_bf.b])
        T.op("pool", lambda: G.memset(neghalf.t[:], -0.5), writes=[neghalf.b])
        sg_v = sgla_d.rearrange("s (b hp) k v -> hp k s b v", hp=2)

        def load_S0_group(grp, dst):
            dv = dst.t[:].rearrange("p (s b v) -> p s b v", s=4, b=2)
            for hp in range(2):
                for b in range(2):
                    T.dma("sp", dv[hp * 64:(hp + 1) * 64, :, b, :], sg_v[hp, :, grp * 4:(grp + 1) * 4, b, :], writes=[dst.b])
            return dv

        def emit_consts():
            T.dma("sp", tri.t[:], tri_d, writes=[tri.b])
            T.dma("sp", mask.t[:], mask_d, writes=[mask.b])
            T.dma("sp", ind.t[:], ind_d, writes=[ind.b])
            T.dma("act", pscale.t[:], pscale_d, writes=[pscale.b])
            T.dma("act", ng.t[:], ng_d, writes=[ng.b])
            T.dma("act", ln_g.t[:], ln_g_d.partition_broadcast(128), writes=[ln_g.b])
            T.dma("act", ln_b.t[:], ln_b_d.partition_broadcast(128), writes=[ln_b.b])
            def nstage():
                sv, sbuf_, own = stg[stg_i[0] % len(stg)]
                stg_i[0] += 1
                return sv, sbuf_, ([sbuf_] + ([own] if own is not None else []))
            sv, sb_, rd = nstage()
            wmv = sv[:, 0:512].rearrange("p (g e) -> p g e", g=4)
            T.dma("sp", wmv, w_map_d.rearrange("g c e -> c g e"), writes=[sb_])
            T.op("dve", lambda wmv=wmv: V.tensor_copy(out=w_map_bf.t[:], in_=wmv), reads=rd, writes=[w_map_bf.b])
            sv, sb_, rd = nstage()
            wav = sv[:, 0:256]
            T.op("pool", lambda wav=wav: G.memset(wav, 0.0), reads=rd[1:], writes=[sb_])
            T.dma("sp", sv[0:16, 0:256], w_a2_d, writes=[sb_])
            T.dma("sp", sv[32:33, 0:256], b_a_d, writes=[sb_])
            T.op("dve", lambda wav=wav: V.tensor_copy(out=w_a2_bf.t[:], in_=wav), reads=rd, writes=[w_a2_bf.b])
            sv, sb_, rd = nstage()
            T.op("pool", lambda sv=sv: G.memset(sv, 0.0), reads=rd[1:], writes=[sb_])
            for hh in range(2):
                T.dma("sp", sv[0:120, hh * 512:(hh + 1) * 512], spool_d[8 * hh:8 * hh + 8].rearrange("s j c -> (s j) c"), writes=[sb_])
            T.op("dve", lambda sv=sv: V.tensor_copy(out=bufS_bf.t[:], in_=sv.rearrange("p (h c) -> p h c", h=2)), reads=rd, writes=[bufS_bf.b])
            for g in range(4):
                sv, sb_, rd = nstage()
                bv = sv[:, 0:768].rearrange("p (k t) -> p k t", k=6)
                T.dma("sp", bv, bands_d[:, g, :, :], writes=[sb_])
                if g % 2 == 0:
                    T.op("dve", lambda g=g, bv=bv: V.tensor_copy(out=bands.t[:, g, :, :], in_=bv), reads=rd, writes=[bands.b])
                else:
                    T.op("dve", lambda g=g, bv=bv: V.tensor_copy(out=bands.t[:, g, :, :], in_=bv), reads=rd, writes=[bands.b])
            T.dma("act", pool_s_d[:, 0:7, :], spool_d[:, 8:15, :])

        col = {"u": 0, "gp": 512, "q": 1024, "k": 1280, "v": 1536, "gg": 2048, "fa": 2560}
        wg_b = {nm: [Buf("w_in_%s%d" % (nm, i)) for i in range(4)] for nm in ("qk", "fa", "u", "v", "gp", "gg")}
        wo_b = [Buf("w_out_%d" % k) for k in range(8)]
        stg = [(stage_t[:, 0, 0:1024], stage_b[0], None), (stage_t[:, 1, 0:1024], stage_b[1], None)]
        for own in (E[0], Kt32[0]):
            stg.append((own.t[:].rearrange("p a b -> p (a b)"), Buf(own.b.name + "_st"), own.b))
        for own in (yo[0], yo[1]):
            stg.append((own.t[:], Buf(own.b.name + "_st"), own.b))
        stg_i = [0]
        cast_eng = [0]

        def stream_cast(src_ap, shape3, dst_ap, dst_buf, eng):
            sv, sbuf_, own = stg[stg_i[0] % len(stg)]
            stg_i[0] += 1
            a_, b_ = shape3
            v = sv[:, 0:a_ * b_].rearrange("p (a b) -> p a b", a=a_)
            T.dma("sp", v, src_ap, writes=[sbuf_])
            rd = [sbuf_] + ([own] if own is not None else [])
            T.op("dve", lambda: V.tensor_copy(out=dst_ap, in_=v), reads=rd, writes=[dst_buf])

        def load_x(ti):
            T.dma("sp", xa[ti % NX].t[:], x_d[ti * 128:(ti + 1) * 128, :], writes=[xa[ti % NX].b])

        xsched = []
        for j in range(4):
            c = [4 * 0 + j, 4 * 1 + j, 4 * 0 + j, 4 * 2 + j, 4 * 1 + j, 4 * 3 + j, 4 * 2 + j]
            if j == 0:
                c.append(16)
            c.append(4 * 3 + j)
            if j == 0:
                c.append(16)
            xsched.append(c)
        xpos = [0, 0, 0, 0]

        def x_advance(slot):
            cur = xsched[slot][xpos[slot]]
            xpos[slot] += 1
            if xpos[slot] < len(xsched[slot]):
                nxt = xsched[slot][xpos[slot]]
                if nxt != cur:
                    load_x(nxt)

        for ti in range(4):
            load_x(ti)
        w_in_v = w_in_d.rearrange("(k p) c -> p k c", p=128)
        groups = {"qk": (1024, 512), "fa": (2560, 16), "u": (0, 512), "v": (1536, 512), "gp": (512, 512), "gg": (2048, 512)}

        def stream_group(nm):
            c0, wd = groups[nm]
            if nm == "fa":
                stream_cast(w_in_v[:, :, c0:c0 + 16], (8, 16), w_in_bf.t[:, :, c0:c0 + 16], wg_b[nm][0], "dve")
                return
            for kk in range(4):
                cast_eng[0] += 1
                stream_cast(w_in_v[:, 2 * kk:2 * kk + 2, c0:c0 + wd], (2, wd), w_in_bf.t[:, 2 * kk:2 * kk + 2, c0:c0 + wd], wg_b[nm][kk],
                            "act" if cast_eng[0] % 2 == 0 else "dve")

        def stream_w_out():
            for kc in range(8):
                stream_cast(w_out_d[kc * 128:(kc + 1) * 128, :].rearrange("p (a b) -> p a b", a=1), (1, 1024),
                            w_out_bf.t[:, kc:kc + 1, :], wo_b[kc], "act" if kc % 2 == 0 else "dve")
        wcol_b = {"u": wg_b["u"], "gp": wg_b["gp"], "q": wg_b["qk"], "k": wg_b["qk"], "v": wg_b["v"], "gg": wg_b["gg"], "fa": wg_b["fa"][0:1]}

        sts = [(0, 4), (4, 4), (8, 4), (12, 4), (16, 1)]
        xTb = xT[0]
        go_v = gla_s_d.rearrange("s (b hp) k v -> hp k s b v", hp=2)
        pj_i = [0]
        prologue_mode = [False]

        def pbank():
            bk = banks[pj_i[0] % 2]
            pj_i[0] += 1
            return bk
        ms_i = [0]

        def mbank():
            bk = banks[2 + ms_i[0] % 4]
            ms_i[0] += 1
            return bk
        BK_A, BK_O, BK_S, BK_Q = banks[2], banks[3], banks[4], banks[5]
        c1_early = [None]
        E_b = [Buf("E_b0"), Buf("E_b1")]
        BK_O2 = [banks[3], banks[5]]

        def unit_A(t0, j):
            ti = t0 + j
            xs = xa[ti % NX]

            def tr():
                ins = None
                for kc in range(8):
                    ins = PE.transpose(out=pbig[0].t[:, kc * 128:(kc + 1) * 128], in_=xs.t[:, kc * 128:(kc + 1) * 128], identity=ident.t[:])
                return ins
            T.op("pe", tr, reads=[xs.b, ident.b], writes=[pbig[0].b])
            T.op("act", lambda: A.copy(out=xTb.t[:, :, j * 128:(j + 1) * 128], in_=pbig[0].t[:].rearrange("p (k t) -> p k t", k=8)),
                 reads=[pbig[0].b], writes=[xTb.b])
            x_advance(ti % NX)

        def proj_fm(W, nm, off, m, evac):
            bk = pbank()
            c0 = col[nm] + off

            def f():
                ins = None
                for kc in range(8):
                    ins = PE.matmul(bk.t[0:m, 0:W], lhsT=w_in_bf.t[:, kc, c0:c0 + m], rhs=xTb.t[:, kc, 0:W], start=(kc == 0), stop=(kc == 7))
                return ins
            T.op("pe", f, reads=wcol_b[nm] + [xTb.b], writes=[bk.b])
            evac(bk)

        def unit_fa(W):
            proj_fm(W, "fa", 0, 16, lambda bk: T.op("act", lambda: A.copy(out=faT[0].t[0:16, 0:W], in_=bk.t[0:16, 0:W]), reads=[bk.b], writes=[faT[0].b]))

        def unit_q(W, b):
            proj_fm(W, "q", b * 128, 128, lambda bk: T.op("act", lambda: A.copy(out=qT[0].t[:, b, 0:W], in_=bk.t[:, 0:W]), reads=[bk.b], writes=[qT[0].b]))

        def unit_k(W, b):
            proj_fm(W, "k", b * 128, 128, lambda bk: T.op("act", lambda: A.copy(out=kT[0].t[:, b, 0:W], in_=bk.t[:, 0:W]), reads=[bk.b], writes=[kT[0].b]))

        def unit_gp(W):
            for g in range(4):
                proj_fm(W, "gp", g * 128, 128, lambda bk, g=g: T.op("act", lambda: A.activation(out=sgp[0].t[:, g, 0:W], in_=bk.t[:, 0:W], func=AF.Silu),
                                                                     reads=[bk.b], writes=[sgp[0].b]))

        def unit_gg(W):
            for h in range(4):
                proj_fm(W, "gg", h * 128, 128, lambda bk, h=h: T.op("act", lambda: A.activation(out=sgg[0].t[:, h, 0:W], in_=bk.t[:, 0:W], func=AF.Silu),
                                                                     reads=[bk.b], writes=[sgg[0].b]))
            for h in range(4):
                T.op("dve", lambda h=h: V.tensor_scalar(out=sgg[0].t[:, h, 0:W], in0=sgg[0].t[:, h, 0:W], scalar1=ng.t[:, h:h + 1], scalar2=None, op0=ALU.mult),
                     reads=[sgg[0].b, ng.b], writes=[sgg[0].b])

        def unit_tm(t0, j, nm, samp):
            ti = t0 + j
            bk = pbank()

            def f():
                ins = None
                for kc in range(8):
                    ins = PE.matmul(bk.t[:, 0:512], lhsT=xTb.t[:, kc, j * 128:(j + 1) * 128], rhs=w_in_bf.t[:, kc, col[nm]:col[nm] + 512],
                                    start=(kc == 0), stop=(kc == 7))
                return ins
            T.op("pe", f, reads=wcol_b[nm] + [xTb.b], writes=[bk.b])
            d = (u_bf[ti % 5] if nm == "u" else v_bf[ti % 4])
            T.op("act", lambda: A.copy(out=d.t[:], in_=bk.t[:, 0:512]), reads=[bk.b], writes=[d.b])
            if nm == "u" and (ti == 15 or samp):
                po = yo[0]
                T.op("act", lambda: A.copy(out=po.t[:, 0:512], in_=bk.t[:, 0:512]), reads=[bk.b], writes=[po.b])
                if ti == 15:
                    T.dma("sp", pool_p_d, po.t[113:128, 0:512], reads=[po.b])
                else:
                    for seq in range(16):
                        T.dma("sp", pool_s_d[seq, 7:15, :], po.t[seq * 8:(seq + 1) * 8, 0:512], reads=[po.b])

        def early_units(sti):
            t0, nt = sts[sti]
            W = 128 * nt
            samp = (sti == 4)
            us = [("A", lambda j=j: unit_A(t0, j)) for j in range(nt)]
            us.append(("fa", lambda: unit_fa(W)))
            us += [("q", lambda b=b: unit_q(W, b)) for b in range(2)]
            us += [("k", lambda b=b: unit_k(W, b)) for b in range(2)]
            us += [("u", lambda j=j: unit_tm(t0, j, "u", samp)) for j in range(nt)]
            us.append(("gp", lambda: unit_gp(W)))
            return us

        def late_units(sti):
            t0, nt = sts[sti]
            W = 128 * nt
            samp = (sti == 4)
            return [lambda j=j: unit_tm(t0, j, "v", samp) for j in range(nt)] + [lambda: unit_gg(W)]

        fillers = []
        late_fillers = []

        def lfill(n=1):
            for _ in range(n):
                if late_fillers:
                    late_fillers.pop(0)()

        def fill(n=1, kinds=None):
            for _ in range(n):
                if fillers and (kinds is None or fillers[0][0] in kinds):
                    fillers.pop(0)[1]()

        for nm_ in ("fa", "qk", "u"):
            stream_group(nm_)
        emit_consts()
        for nm_ in ("gp", "v", "gg"):
            stream_group(nm_)
        stream_w_out()
        prologue_mode[0] = True
        eu = early_units(0)
        lu = late_units(0)
        for k_, u_ in eu:
            u_()
        for u_ in lu:
            u_()
        prologue_mode[0] = False

        for sti, (t0, nt) in enumerate(sts):
            W = 128 * nt
            samp = (sti == 4)
            ci = 1 if samp else 0
            if sti + 1 < len(sts):
                fillers.extend(early_units(sti + 1))
            def C_part1(sti):
                t0, nt = sts[sti]
                W = 128 * nt
                samp = (sti == 4)
                cbk = []
                for g in range(4):
                    bk = banks[2 + g]
                    cbk.append(bk)

                    def f(bk=bk, g=g):
                        ins = None
                        for j in range(nt):
                            ti = t0 + j
                            ucur = u_bf[ti % 5]
                            o = bk.t[:, j * 128:(j + 1) * 128]
                            lw = ucur.t[:, g * 128:(g + 1) * 128]
                            if samp:
                                PE.matmul(o, lhsT=lw, rhs=bands.t[:, g, 3, :], start=True, stop=False)
                                PE.matmul(o, lhsT=bufS_bf.t[:, 0, g * 128:(g + 1) * 128], rhs=bands.t[:, g, 4, :], start=False, stop=False)
                                ins = PE.matmul(o, lhsT=bufS_bf.t[:, 1, g * 128:(g + 1) * 128], rhs=bands.t[:, g, 5, :], start=False, stop=True)
                            elif ti == 0:
                                ins = PE.matmul(o, lhsT=lw, rhs=bands.t[:, g, 2, :], start=True, stop=True)
                            else:
                                PE.matmul(o, lhsT=lw, rhs=bands.t[:, g, 0, :], start=True, stop=False)
                                ins = PE.matmul(o, lhsT=u_bf[(ti - 1) % 5].t[:, g * 128:(g + 1) * 128], rhs=bands.t[:, g, 1, :], start=False, stop=True)
                        return ins
                    urd = [bands.b] + [u_bf[(t0 + j) % 5].b for j in range(nt)]
                    if t0 > 0 and not samp:
                        urd.append(u_bf[(t0 - 1) % 5].b)
                    if samp:
                        urd.append(bufS_bf.b)
                    T.op("pe", f, reads=urd, writes=[bk.b])
                    T.op("act", lambda bk=bk, g=g: A.copy(out=pooledT[0].t[:, g, 0:W], in_=bk.t[:, 0:W]), reads=[bk.b], writes=[pooledT_g[g]])
                return cbk
            if c1_early[0] is None:
                cbk = C_part1(sti)
            else:
                cbk = c1_early[0]
                c1_early[0] = None
            lfill(2)
            for g in range(4):
                bk2 = cbk[g]
                T.op("pe", lambda bk2=bk2, g=g: PE.matmul(bk2.t[:, 0:W], lhsT=w_map_bf.t[:, g, :], rhs=pooledT[0].t[:, g, 0:W], start=True, stop=True),
                     reads=[w_map_bf.b, pooledT_g[g]], writes=[bk2.b])
                T.op("dve", lambda bk2=bk2, g=g: V.scalar_tensor_tensor(out=mixT[0].t[:, g, 0:W], in0=bk2.t[:, 0:W], scalar=pscale.t[:, g:g + 1],
                                                                        in1=sgp[0].t[:, g, 0:W], op0=ALU.mult, op1=ALU.mult),
                     reads=[bk2.b, pscale.b, sgp[0].b], writes=[mixT[0].b])
                if g % 2 == 1:
                    lfill(1)
            for jp in range(0, nt, 2):
                npair = min(2, nt - jp)
                wp = 256 * npair
                bk = mbank()

                def fg(bk=bk, jp=jp, npair=npair):
                    ins = None
                    for jj in range(npair):
                        ins = PE.matmul(bk.t[:, jj * 256:(jj + 1) * 256], lhsT=faT[0].t[:, (jp + jj) * 128:(jp + jj + 1) * 128], rhs=w_a2_bf.t[:],
                                        start=True, stop=True)
                    return ins
                T.op("pe", fg, reads=[faT[0].b, w_a2_bf.b], writes=[bk.b])
                lp = LtP[((t0 + jp) % 4) // 2]
                T.op("act", lambda bk=bk, wp=wp: A.activation(out=expg[0].t[:, 0:wp], in_=bk.t[:, 0:wp], func=AF.Exp, scale=-1.0),
                     reads=[bk.b], writes=[expg[0].b])
                T.op("act", lambda lp=lp, wp=wp: A.activation(out=lp.t[:, 0:wp], in_=expg[0].t[:, 0:wp], func=AF.Ln, bias=1.0),
                     reads=[expg[0].b], writes=[lp.b])
            lfill(len(late_fillers))
            fill(2, ("A",))
            for b in range(2):
                bk = mbank()

                def f(bk=bk, b=b):
                    ins = None
                    for j in range(nt):
                        ins = PE.matmul(bk.t[:, j * 128:(j + 1) * 128], lhsT=LtP[((t0 + j) % 4) // 2].t[:, ((t0 + j) % 2) * 256 + b * 128:((t0 + j) % 2) * 256 + (b + 1) * 128], rhs=tri.t[:, ci, :],
                                        start=True, stop=True)
                    return ins
                T.op("pe", f, reads=[tri.b] + [LtP[((t0 + j) % 4) // 2].b for j in range(nt)], writes=[bk.b])
                T.op("act", lambda bk=bk, b=b: A.activation(out=E[0].t[:, b, 0:W], in_=bk.t[:, 0:W], func=AF.Exp), reads=[bk.b], writes=[E[0].b, E_b[b]])
                T.op("act", lambda bk=bk, b=b: A.activation(out=Kt32[0].t[:, b, 0:W], in_=bk.t[:, 0:W], func=AF.Exp, scale=-1.0), reads=[bk.b], writes=[Kt32[0].b])
            for b in range(2):
                for hp in range(2):
                    ps_ = slice(hp * 64, (hp + 1) * 64)
                    T.op("dve", lambda b=b, hp=hp, ps_=ps_: V.scalar_tensor_tensor(out=Qpad[0].t[ps_, b, hp, 0:W], in0=qT[0].t[ps_, b, 0:W], scalar=0.125,
                                                                                  in1=E[0].t[ps_, b, 0:W], op0=ALU.mult, op1=ALU.mult),
                         reads=[qT[0].b, E_b[b]], writes=[Qpad[0].b])
            T.op("pool", lambda: G.tensor_tensor(out=KtT[0].t[:, :, 0:W], in0=kT[0].t[:, :, 0:W], in1=Kt32[0].t[:, :, 0:W], op=ALU.mult),
                 reads=[kT[0].b, Kt32[0].b], writes=[KtT[0].b])
            for b in range(2):
                if samp:
                    ebc = E[0].t[:, b, 7:128:8].unsqueeze(2).to_broadcast([128, 16, 8])
                    kv = Kt32[0].t[:, b, 0:128].rearrange("p (s t) -> p s t", t=8)
                else:
                    ebc = E[0].t[:, b, 127:512:128].unsqueeze(2).to_broadcast([128, 4, 128])
                    kv = Kt32[0].t[:, b, :].rearrange("p (s t) -> p s t", t=128)
                T.op("pool", lambda kv=kv, ebc=ebc: G.tensor_tensor(out=kv, in0=kv, in1=ebc, op=ALU.mult),
                     reads=[Kt32[0].b, E[0].b], writes=[Kt32[0].b])
            T.op("pool", lambda: G.tensor_tensor(out=Kt32[0].t[:, :, 0:W], in0=kT[0].t[:, :, 0:W], in1=Kt32[0].t[:, :, 0:W], op=ALU.mult),
                 reads=[kT[0].b, Kt32[0].b], writes=[Kt32[0].b])
            fill(2, ("A",))
            fill(1, ("fa",))
            fill(2, ("q",))
            for j in range(nt):
                ti = t0 + j
                bk = mbank()

                def f(bk=bk, j=j):
                    ins = None
                    for b in range(2):
                        ins = PE.transpose(out=bk.t[:, b * 128:(b + 1) * 128], in_=Kt32[0].t[:, b, j * 128:(j + 1) * 128], identity=ident.t[:])
                    return ins
                T.op("pe", f, reads=[Kt32[0].b, ident.b], writes=[bk.b])
                if samp:
                    T.op("act", lambda bk=bk, ti=ti: A.copy(out=Kh_bf[ti % 4].t[:, 0:256], in_=bk.t[:, 0:256]), reads=[bk.b], writes=[Kh_bf[ti % 4].b])
                else:
                    kdst = bass.AP(Kh_bf[ti % 4].t, 0, [[512, 128], [256, 2], [192, 2], [1, 64]])
                    ksrc = bass.AP(bk.t, 0, [[512, 128], [128, 2], [64, 2], [1, 64]])
                    T.op("act", lambda kdst=kdst, ksrc=ksrc: A.copy(out=kdst, in_=ksrc), reads=[bk.b], writes=[Kh_bf[ti % 4].b])
            lfill(len(late_fillers))
            def E1a(j):
                ti = t0 + j
                tsl = slice(j * 128, (j + 1) * 128)
                a2 = ti % 2

                def fA():
                    ins = None
                    for b in range(2):
                        ins = PE.matmul(BK_A.t[:, b * 256:(b + 1) * 256].rearrange("p (h t) -> p h t", h=2), lhsT=KtT[0].t[:, b, tsl],
                                        rhs=Qpad[0].t[:, b, :, tsl], start=True, stop=True)
                    return ins
                T.op("pe", fA, reads=[KtT[0].b, Qpad[0].b], writes=[BK_A.b])
                T.op("dve", lambda: V.tensor_tensor(out=AT_bf[a2].t[:].rearrange("p (h t) -> p h t", h=4),
                                                    in0=BK_A.t[:, 0:512].rearrange("p (h t) -> p h t", h=4),
                                                    in1=mask.t[:, 0, :].unsqueeze(1).to_broadcast([128, 4, 128]), op=ALU.mult),
                     reads=[BK_A.b, mask.b], writes=[AT_bf[a2].b])

            def E1b(j):
                ti = t0 + j
                tsl = slice(j * 128, (j + 1) * 128)
                a2 = ti % 2
                bkO = BK_O2[ti % 2]
                sprev = S_bf[ti % 2]
                snext = S_bf[(ti + 1) % 2]

                def fO():
                    ins = None
                    for b in range(2):
                        ob = bkO.t[:, b * 256:(b + 1) * 256].rearrange("p (h t) -> p h t", h=2)
                        PE.matmul(ob, lhsT=sprev.t[:, b, :], rhs=Qpad[0].t[:, b, :, tsl], start=(b == 0), stop=False, skip_group_check=True)
                    for h in range(4):
                        ins = PE.matmul(bkO.t[:, h * 128:(h + 1) * 128], lhsT=v_bf[ti % 4].t[:, h * 128:(h + 1) * 128],
                                        rhs=AT_bf[a2].t[:, h * 128:(h + 1) * 128], start=False, stop=True, skip_group_check=True)
                    return ins
                T.op("pe", fO, reads=[Qpad[0].b, AT_bf[a2].b, v_bf[ti % 4].b, sprev.b], writes=[bkO.b])

                def fS():
                    ins = None
                    for b in range(2):
                        for hp in range(2):
                            ins = PE.matmul(BK_S.t[:, b * 128:(b + 1) * 128], lhsT=Kh_bf[ti % 4].t[:, (2 * b + hp) * 128:(2 * b + hp + 1) * 128],
                                            rhs=v_bf[ti % 4].t[:, (2 * b + hp) * 128:(2 * b + hp + 1) * 128], start=(hp == 0), stop=(hp == 1))
                    return ins
                T.op("pe", fS, reads=[Kh_bf[ti % 4].b, v_bf[ti % 4].b], writes=[BK_S.b])
                for b in range(2):
                    T.op("dve", lambda b=b: V.scalar_tensor_tensor(
                        out=S.t[:, b, :], in0=S.t[:, b, :], scalar=E[0].t[:, b, j * 128 + 127:j * 128 + 128],
                        in1=BK_S.t[:, b * 128:(b + 1) * 128], op0=ALU.mult, op1=ALU.add),
                        reads=[S.b, E[0].b, BK_S.b], writes=[S.b])
                T.op("act", lambda: A.copy(out=snext.t[:], in_=S.t[:]), reads=[S.b], writes=[snext.b])
                if ti == 15:
                    for b in range(2):
                        for hp in range(2):
                            T.dma("sp", gla_p_d[2 * b + hp], S.t[hp * 64:(hp + 1) * 64, b, :], reads=[S.b])

            def E2a(j):
                ti = t0 + j
                a2 = ti % 2
                bkO = BK_O2[ti % 2]
                T.op("act", lambda: A.activation(out=osq[a2].t[:], in_=bkO.t[:, 0:512], func=AF.Square), reads=[bkO.b], writes=[osq[a2].b])

            def E2b_pe(j):
                ti = t0 + j
                a2 = ti % 2
                bkQ = BK_A
                T.op("pe", lambda: PE.matmul(bkQ.t[:, 0:512], lhsT=ones_bf.t[:], rhs=osq[a2].t[:], start=True, stop=True),
                     reads=[ones_bf.b, osq[a2].b], writes=[bkQ.b])
                T.op("act", lambda: A.activation(out=lnv1.t, in_=bkQ.t[:, 0:512], func=AF.Ln, scale=1.0 / 128.0, bias=RMS_EPS),
                     reads=[bkQ.b], writes=[lnv1.b])

            def E2b(j):
                ti = t0 + j
                tsl = slice(j * 128, (j + 1) * 128)
                a2 = ti % 2
                bkO = BK_O2[ti % 2]
                bkQ = BK_A
                T.op("act", lambda: A.activation(out=lnv1.t, in_=lnv1.t, func=AF.Exp, scale=-0.5), reads=[lnv1.b], writes=[lnv1.b])
                T.op("dve", lambda: V.tensor_tensor(out=on1.t, in0=bkO.t[:, 0:512], in1=lnv1.t, op=ALU.mult),
                     reads=[bkO.b, lnv1.b], writes=[on1.b])
                T.op("pool", lambda: G.tensor_tensor(out=mixT[0].t[:, 4:8, tsl], in0=on1.t.rearrange("p (h t) -> p h t", h=4),
                                                      in1=sgg[0].t[:, :, tsl], op=ALU.mult),
                     reads=[on1.b, sgg[0].b], writes=[mixTg4[j]])

            def Fst_pe(j):
                tsl = slice(j * 128, (j + 1) * 128)

                def fY():
                    ins = None
                    for n in range(2):
                        for e in range(8):
                            ins = PE.matmul(pbig[0].t[:, n * 512:(n + 1) * 512], lhsT=mixT[0].t[:, e, tsl], rhs=w_out_bf.t[:, e, n * 512:(n + 1) * 512],
                                            start=(e == 0), stop=(e == 7))
                    return ins
                T.op("pe", fY, reads=[mixT[0].b, mixTg4[j]] + wo_b, writes=[pbig[0].b])

            def Fst(j):
                ti = t0 + j
                a2 = ti % 2
                xrs = xa[ti % NX]
                T.op("dve", lambda: V.scalar_tensor_tensor(out=hres1.t, in0=xrs.t[:], scalar=ALPHA, in1=pbig[0].t[:], op0=ALU.mult, op1=ALU.add),
                     reads=[xrs.b, pbig[0].b], writes=[hres1.b])
                x_advance(ti % NX)
                T.op("dve", lambda: V.bn_stats(out=stats[a2].t[:, 0:6], in_=hres1.t[:, 0:512]), reads=[hres1.b], writes=[stats[a2].b])
                T.op("dve", lambda: V.bn_stats(out=stats[a2].t[:, 6:12], in_=hres1.t[:, 512:1024]), reads=[hres1.b], writes=[stats[a2].b])
                T.op("dve", lambda: V.bn_aggr(out=mv[a2].t[:], in_=stats[a2].t[:]), reads=[stats[a2].b], writes=[mv[a2].b])
                T.op("pool", lambda: G.tensor_scalar(out=lr[a2].t[:, 0:1], in0=mv[a2].t[:, 1:2], scalar1=LN_EPS, scalar2=None, op0=ALU.add),
                     reads=[mv[a2].b], writes=[lr[a2].b])
                T.op("pool", lambda: G.tensor_tensor(out=lr[a2].t[:, 0:1], in0=lr[a2].t[:, 0:1], in1=neghalf.t[:, 0:1], op=ALU.pow),
                     reads=[lr[a2].b, neghalf.b], writes=[lr[a2].b])
                T.op("dve", lambda: V.tensor_scalar(out=yo[a2].t[:], in0=hres1.t, scalar1=mv[a2].t[:, 0:1], scalar2=lr[a2].t[:, 0:1],
                                                    op0=ALU.subtract, op1=ALU.mult),
                     reads=[hres1.b, mv[a2].b, lr[a2].b], writes=[yo[a2].b])
                T.op("dve", lambda: V.tensor_tensor(out=yo[a2].t[:], in0=yo[a2].t[:], in1=ln_g.t[:], op=ALU.mult), reads=[yo[a2].b, ln_g.b], writes=[yo[a2].b])
                T.op("pool", lambda: G.tensor_tensor(out=yo[a2].t[:], in0=yo[a2].t[:], in1=ln_b.t[:], op=ALU.add), reads=[yo[a2].b, ln_b.b], writes=[yo[a2].b])
                T.dma("sp", y_d[ti * 128:(ti + 1) * 128, :], yo[a2].t[:], reads=[yo[a2].b])

            if not samp:
                E1a(0)
                fill(1)
                E1b(0)
                E2a(0)
                for j in range(nt):
                    if j >= 1:
                        Fst_pe(j - 1)
                    E2b_pe(j)
                    if j + 1 < nt:
                        E1a(j + 1)
                    E2b(j)
                    if j in (0, 1, 2):
                        fill(1)
                    if j + 1 < nt:
                        E1b(j + 1)
                        E2a(j + 1)
                    if j in (0, 1, 3):
                        fill(1)
                    if j >= 1:
                        Fst(j - 1)
                if sti + 1 < len(sts):
                    while fillers:
                        fillers.pop(0)[1]()
                    c1_early[0] = C_part1(sti + 1)
                Fst_pe(nt - 1)
                Fst(nt - 1)
            if samp:
                ti = 16
                a2 = 0
                tsl = slice(0, 128)
                S0blk = [S0b.t[:], sgp[0].t[:].rearrange("p g (s v) -> p (g s) v", v=128)]
                S0blk_b = [S0b_h[0], sgp[0].b]
                for b in range(2):
                    T.dma_batch("sp", [(S0blk[b][hp * 64:(hp + 1) * 64, :, :], sg_v[hp, :, :, b, :]) for hp in range(2)], writes=[S0blk_b[b]])
                gbuf = [yo[0], yo[1], xa[1], xa[2]]
                gdv = [g_.t[:].rearrange("p (s b v) -> p s b v", s=4, b=2) for g_ in gbuf]
                for grp in range(4):
                    T.dma_batch("sp", [(gdv[grp][hp * 64:(hp + 1) * 64, :, b, :], sg_v[hp, :, grp * 4:(grp + 1) * 4, b, :])
                                       for hp in range(2) for b in range(2)], writes=[gbuf[grp].b])
                xTv = xT[0].t[:].rearrange("p k t -> p (k t)")
                S0bf = [xTv[:, b * 2048:(b + 1) * 2048].rearrange("p (s v) -> p s v", v=128) for b in range(2)]
                for b in range(2):
                    T.op("act", lambda b=b: A.copy(out=S0bf[b], in_=S0blk[b]), reads=[S0blk_b[b]], writes=[xT[0].b])
                kh = Kh_bf[16 % 4]
                qbs = [[Buf("sq_%d_%d" % (g_, q_)) for q_ in range(4)] for g_ in range(4)]
                for g_ in range(4):
                    ebc = bass.AP(E[0].t, g_ * 32 + 7, [[1024, 128], [8, 4], [512, 2], [0, 128]])
                    T.op("pool", lambda dv=gdv[g_], ebc=ebc: G.tensor_tensor(out=dv, in0=dv, in1=ebc, op=ALU.mult),
                         reads=[gbuf[g_].b, E[0].b], writes=[gbuf[g_].b] + qbs[g_])

                def supd(grp):
                    km = Khm[grp % 2]
                    T.op("dve", lambda km=km, grp=grp: V.tensor_tensor(out=km.t[:], in0=kh.t[:, 0:256].unsqueeze(1).to_broadcast([128, 4, 256]),
                                                                       in1=ind.t[:, grp * 4:(grp + 1) * 4].unsqueeze(2).to_broadcast([128, 4, 256]), op=ALU.mult),
                         reads=[kh.b, ind.b], writes=[km.b])
                    sdst = gbuf[grp]
                    dv = gdv[grp]
                    qb = qbs[grp]
                    for qq in range(4):
                        bk = mbank()

                        def fS(bk=bk, qq=qq, km=km):
                            ins = None
                            for b in range(2):
                                ins = PE.matmul(bk.t[:, b * 256:(b + 1) * 256], lhsT=km.t[:, qq, b * 128:(b + 1) * 128],
                                                rhs=v_bf[16 % 4].t[:, b * 256:(b + 1) * 256], start=True, stop=True)
                            return ins
                        T.op("pe", fS, reads=[km.b, v_bf[16 % 4].b], writes=[bk.b])
                        bv = bk.t[:, 0:512].rearrange("p (b h v) -> p b h v", b=2, h=2)
                        for hp in range(2):
                            ps_ = slice(hp * 64, (hp + 1) * 64)
                            T.op("dve", lambda bv=bv, qq=qq, hp=hp, ps_=ps_, dv=dv: V.tensor_tensor(
                                out=dv[ps_, qq, :, :], in0=dv[ps_, qq, :, :], in1=bv[ps_, :, hp, :], op=ALU.add),
                                reads=[bk.b], writes=[qb[qq]] if hp == 0 else [qb[qq]])
                    T.dma_batch("sp", [(go_v[hp, :, grp * 4:(grp + 1) * 4, b, :], dv[hp * 64:(hp + 1) * 64, :, b, :])
                                       for hp in range(2) for b in range(2)], reads=qb + [sdst.b])

                def fA():
                    ins = None
                    for b in range(2):
                        ins = PE.matmul(BK_A.t[:, b * 256:(b + 1) * 256].rearrange("p (h t) -> p h t", h=2), lhsT=KtT[0].t[:, b, tsl],
                                        rhs=Qpad[0].t[:, b, :, tsl], start=True, stop=True)
                    return ins
                T.op("pe", fA, reads=[KtT[0].b, Qpad[0].b], writes=[BK_A.b])
                T.op("dve", lambda: V.tensor_tensor(out=AT_bf[a2].t[:].rearrange("p (h t) -> p h t", h=4),
                                                    in0=BK_A.t[:, 0:512].rearrange("p (h t) -> p h t", h=4),
                                                    in1=mask.t[:, 1, :].unsqueeze(1).to_broadcast([128, 4, 128]), op=ALU.mult),
                     reads=[BK_A.b, mask.b], writes=[AT_bf[a2].b])
                supd(0)
                bkO = BK_O

                def fO():
                    ins = None
                    first = True
                    for b in range(2):
                        for seq in range(16):
                            for hp in range(2):
                                c0 = b * 256 + hp * 128 + seq * 8
                                PE.matmul(bkO.t[:, c0:c0 + 8], lhsT=S0bf[b][:, seq, :], rhs=Qpad[0].t[:, b, hp, seq * 8:(seq + 1) * 8],
                                          start=first, stop=False, skip_group_check=True)
                                first = False
                    for h in range(4):
                        ins = PE.matmul(bkO.t[:, h * 128:(h + 1) * 128], lhsT=v_bf[16 % 4].t[:, h * 128:(h + 1) * 128],
                                        rhs=AT_bf[a2].t[:, h * 128:(h + 1) * 128], start=False, stop=True, skip_group_check=True)
                    return ins
                T.op("pe", fO, reads=[Qpad[0].b, xT[0].b, AT_bf[a2].b, v_bf[16 % 4].b], writes=[bkO.b])
                T.op("act", lambda: A.activation(out=osq[a2].t[:], in_=bkO.t[:, 0:512], func=AF.Square), reads=[bkO.b], writes=[osq[a2].b])
                bkQ = BK_Q
                T.op("pe", lambda: PE.matmul(bkQ.t[:, 0:512], lhsT=ones_bf.t[:], rhs=osq[a2].t[:], start=True, stop=True),
                     reads=[ones_bf.b, osq[a2].b], writes=[bkQ.b])
                T.op("act", lambda: A.activation(out=lnv1.t, in_=bkQ.t[:, 0:512], func=AF.Ln, scale=1.0 / 128.0, bias=RMS_EPS),
                     reads=[bkQ.b], writes=[lnv1.b])
                T.op("act", lambda: A.activation(out=lnv1.t, in_=lnv1.t, func=AF.Exp, scale=-0.5), reads=[lnv1.b], writes=[lnv1.b])
                T.op("dve", lambda: V.tensor_tensor(out=on1.t, in0=bkO.t[:, 0:512], in1=lnv1.t, op=ALU.mult),
                     reads=[bkO.b, lnv1.b], writes=[on1.b])
                T.op("dve", lambda: V.tensor_tensor(out=mixT[0].t[:, 4:8, tsl], in0=on1.t.rearrange("p (h t) -> p h t", h=4),
                                                    in1=sgg[0].t[:, :, tsl], op=ALU.mult),
                     reads=[on1.b, sgg[0].b], writes=[mixTg[0], mixTg4[0]])
                supd(1)
                xrs = xa[0]
                yb = xa[3]

                def fY():
                    ins = None
                    for n in range(2):
                        for e in range(8):
                            ins = PE.matmul(pbig[0].t[:, n * 512:(n + 1) * 512], lhsT=mixT[0].t[:, e, tsl], rhs=w_out_bf.t[:, e, n * 512:(n + 1) * 512],
                                            start=(e == 0), stop=(e == 7))
                    return ins
                T.op("pe", fY, reads=[mixT[0].b, mixTg[0]] + wo_b, writes=[pbig[0].b])
                supd(2)
                T.op("dve", lambda: V.scalar_tensor_tensor(out=hres1.t, in0=xrs.t[:], scalar=ALPHA, in1=pbig[0].t[:], op0=ALU.mult, op1=ALU.add),
                     reads=[xrs.b, pbig[0].b], writes=[hres1.b])
                T.op("dve", lambda: V.bn_stats(out=stats[a2].t[:, 0:6], in_=hres1.t[:, 0:512]), reads=[hres1.b], writes=[stats[a2].b])
                T.op("dve", lambda: V.bn_stats(out=stats[a2].t[:, 6:12], in_=hres1.t[:, 512:1024]), reads=[hres1.b], writes=[stats[a2].b])
                T.op("dve", lambda: V.bn_aggr(out=mv[a2].t[:], in_=stats[a2].t[:]), reads=[stats[a2].b], writes=[mv[a2].b])
                T.op("pool", lambda: G.tensor_scalar(out=lr[a2].t[:, 0:1], in0=mv[a2].t[:, 1:2], scalar1=LN_EPS, scalar2=None, op0=ALU.add),
                     reads=[mv[a2].b], writes=[lr[a2].b])
                T.op("pool", lambda: G.tensor_tensor(out=lr[a2].t[:, 0:1], in0=lr[a2].t[:, 0:1], in1=neghalf.t[:, 0:1], op=ALU.pow),
                     reads=[lr[a2].b, neghalf.b], writes=[lr[a2].b])
                T.op("dve", lambda: V.scalar_tensor_tensor(out=lr[a2].t[:, 1:2], in0=mv[a2].t[:, 0:1], scalar=-1.0, in1=lr[a2].t[:, 0:1],
                                                           op0=ALU.mult, op1=ALU.mult),
                     reads=[mv[a2].b, lr[a2].b], writes=[lr[a2].b])
                T.op("dve", lambda: V.tensor_scalar(out=yb.t[:], in0=hres1.t, scalar1=lr[a2].t[:, 0:1], scalar2=lr[a2].t[:, 1:2],
                                                    op0=ALU.mult, op1=ALU.add),
                     reads=[hres1.b, lr[a2].b], writes=[yb.b])
                T.op("pool", lambda: G.tensor_tensor(out=yb.t[:], in0=yb.t[:], in1=ln_g.t[:], op=ALU.mult), reads=[yb.b, ln_g.b], writes=[yb.b])
                T.op("pool", lambda: G.tensor_tensor(out=yb.t[:], in0=yb.t[:], in1=ln_b.t[:], op=ALU.add), reads=[yb.b, ln_b.b], writes=[yb.b])
                T.dma("sp", y_d[ti * 128:(ti + 1) * 128, :], yb.t[:], reads=[yb.b])
                supd(3)
            while fillers:
                fillers.pop(0)[1]()
            if sti + 1 < len(sts):
                late_fillers.extend(late_units(sti + 1))
        T.wait_all_dma("sp")
    return nc


_CACHE = {}


def kernel(x_prompt, x_sample, state_pool, state_gla, w_in, w_pool_map, pool_scale, w_gla_a2,
           b_gla_a, gla_norm_g, w_out, ln_g, ln_b):
    f = lambda a: np.ascontiguousarray(np.asarray(a, dtype=np.float32))
    x_prompt, x_sample, state_pool, state_gla = f(x_prompt), f(x_sample), f(state_pool), f(state_gla)
    if "nc" not in _CACHE:
        _CACHE["nc"] = build_program()
        _CACHE["consts"] = build_consts()
    nc = _CACHE["nc"]
    C = _CACHE["consts"]
    shared = {
        "w_in": f(w_in)[0], "w_map": f(w_pool_map)[0], "pscale": np.ascontiguousarray(f(pool_scale)[0].reshape(4, 128).T), "w_a2": f(w_gla_a2)[0],
        "b_a": f(b_gla_a)[0].reshape(1, 256), "ng": np.ascontiguousarray(f(gla_norm_g)[0].reshape(4, 128).T), "w_out": f(w_out)[0],
        "ln_g": f(ln_g)[0].reshape(1, 1024), "ln_b": f(ln_b)[0].reshape(1, 1024),
        "ident": C["ident"], "tri": C["tri"], "mask": C["mask"], "bands": C["bands"], "ind": C["ind"],
    }
    in_maps = []
    for c in range(NCORES):
        m = dict(shared)
        m["x"] = np.ascontiguousarray(np.concatenate([x_prompt[c], x_sample[16 * c:16 * c + 16].reshape(128, 1024)], axis=0))
        m["spool"] = np.ascontiguousarray(state_pool[0, 16 * c:16 * c + 16])
        m["sgla"] = np.ascontiguousarray(state_gla[0, 16 * c:16 * c + 16])
        in_maps.append(m)
    res = run_bass_kernel_spmd(nc, in_maps, core_ids=list(range(NCORES)))
    R = res.results
    y_prompt = np.stack([R[c]["y"][:2048] for c in range(NCORES)], 0).astype(np.float32)
    y_sample = np.concatenate([R[c]["y"][2048:].reshape(16, 8, 1024) for c in range(NCORES)], 0).astype(np.float32)
    pool_p = np.stack([R[c]["pool_p"] for c in range(NCORES)], 0)[None].astype(np.float32)
    gla_p = np.stack([R[c]["gla_p"] for c in range(NCORES)], 0)[None].astype(np.float32)
    pool_s = np.concatenate([R[c]["pool_s"] for c in range(NCORES)], 0)[None].astype(np.float32)
    gla_s = np.concatenate([R[c]["gla_s"] for c in range(NCORES)], 0)[None].astype(np.float32)
    return (y_prompt, y_sample, pool_p, gla_p, pool_s, gla_s)
```

```python
from contextlib import ExitStack

import numpy as np
import concourse.bass as bass
import concourse.mybir as mybir
from concourse.bass_utils import run_bass_kernel_spmd

F32 = mybir.dt.float32
BF16 = mybir.dt.bfloat16
AF = mybir.ActivationFunctionType
ALU = mybir.AluOpType

NCORES = 8
NTOK = 2176
NTILE = 17
WINS = (2, 4, 8, 16)
ALPHA = 2.0 ** 0.25
LN_EPS = 1e-5
RMS_EPS = 1e-6


def build_consts():
    c = {}
    c["ident"] = np.eye(128, dtype=np.float32)
    s = np.arange(128)[:, None]
    t = np.arange(128)[None, :]
    tri_p = np.where(s <= t, -1.0 / 16.0, 0.0).astype(np.float32)
    same = (s // 8) == (t // 8)
    tri_s = np.where(same & (s <= t), -1.0 / 16.0, 0.0).astype(np.float32)
    c["tri"] = np.ascontiguousarray(np.stack([tri_p, tri_s], axis=1))
    mask_p = (s <= t).astype(np.float32)
    mask_s = (same & (s <= t)).astype(np.float32)
    c["mask"] = np.ascontiguousarray(np.stack([mask_p, mask_s], axis=1))
    bands = np.zeros((128, 4, 6, 128), np.float32)
    for g, w in enumerate(WINS):
        bands[:, g, 0, :] = np.where((s <= t) & (s >= t - w + 1), 1.0 / w, 0.0) - (s == t)
        bands[:, g, 1, :] = np.where((s - 128) >= (t - w + 1), 1.0 / w, 0.0)
        cnt = np.minimum(t + 1, w)
        bands[:, g, 2, :] = np.where((s <= t) & (s >= t - w + 1), 1.0 / cnt, 0.0) - (s == t)
        ts = t % 8
        ss = s % 8
        bands[:, g, 3, :] = np.where(same & (ss <= ts) & (ss >= ts - w + 1), 1.0 / w, 0.0) - (s == t)
        for hh in range(2):
            m = np.zeros((128, 128), np.float32)
            for r in range(120):
                sl, j = divmod(r, 15)
                seq = 8 * hh + sl
                for tt in range(8):
                    if j >= 15 + tt - w + 1:
                        m[r, seq * 8 + tt] = 1.0 / w
            bands[:, g, 4 + hh, :] = m
    c["bands"] = bands
    c["ind"] = ((np.arange(128)[:, None] // 8) == np.arange(16)[None, :]).astype(np.float32)
    return c


class Buf:
    __slots__ = ("name", "w", "r", "psum")

    def __init__(self, name, psum=False):
        self.name = name
        self.w = None
        self.r = {}
        self.psum = psum


class Eng:
    def __init__(self, name, h, sem):
        self.name = name
        self.h = h
        self.sem = sem
        self.cnt = 0
        self.seen = {}
        self.dsems = []
        self.dnext = 0


class Tracker:
    NDSEM = 10

    def __init__(self, nc, stack, dma_queues=("sp", "act", "pool")):
        self.nc = nc
        self.eng = {}
        for name, h in (("pe", nc.tensor), ("act", nc.scalar), ("dve", nc.vector),
                        ("pool", nc.gpsimd), ("sp", nc.sync)):
            sem = stack.enter_context(nc.semaphore("s_" + name))
            self.eng[name] = Eng(name, h, sem)
        for q in dma_queues:
            e = self.eng[q]
            for i in range(self.NDSEM):
                s = stack.enter_context(nc.semaphore("d_%s%d" % (q, i)))
                e.dsems.append([s, 0])

    def _need(self, e, dep):
        if dep is None:
            return
        if dep[0] == "m":
            for d in dep[1]:
                self._need(e, d)
            return
        if dep[0] == "d":
            _, sem, val = dep
            key = ("d", sem.num)
            if e.seen.get(key, 0) >= val:
                return
            e.h.wait_ge(sem, val)
            e.seen[key] = val
            return
        _, oname, cnt = dep
        if e.seen.get(oname, 0) >= cnt:
            return
        e.h.wait_ge(self.eng[oname].sem, cnt)
        e.seen[oname] = cnt

    def op(self, ename, fn, reads=(), writes=()):
        e = self.eng[ename]
        for b in reads:
            self._need(e, b.w)
            if b.psum:
                for k, d in b.r.items():
                    if not (d[0] == "e" and d[1] == ename):
                        self._need(e, d)
        for b in writes:
            self._need(e, b.w)
            for k, d in b.r.items():
                self._need(e, d)
        ins = fn()
        ins.then_inc(e.sem, 1)
        e.cnt += 1
        dep = ("e", ename, e.cnt)
        for b in reads:
            b.r[ename] = dep
        for b in writes:
            b.w = dep
            b.r = {}
        return ins

    def dma(self, ename, out, in_, reads=(), writes=()):
        e = self.eng[ename]
        for b in reads:
            self._need(e, b.w)
        for b in writes:
            self._need(e, b.w)
            for k, d in b.r.items():
                self._need(e, d)
        slot = e.dsems[e.dnext % len(e.dsems)]
        e.dnext += 1
        sem, val = slot
        if val > 0:
            self._need(e, ("d", sem, val))
        e.h.dma_start(out=out, in_=in_).then_inc(sem, 16)
        slot[1] = val + 16
        dep = ("d", sem, val + 16)
        for b in writes:
            b.w = dep
            b.r = {}
        for b in reads:
            b.r[("d", sem.num)] = dep
        return dep

    def dma_batch(self, ename, pairs, reads=(), writes=()):
        e = self.eng[ename]
        for b in reads:
            self._need(e, b.w)
        for b in writes:
            self._need(e, b.w)
            for k, d in b.r.items():
                self._need(e, d)
        deps = []
        for out, in_ in pairs:
            slot = e.dsems[e.dnext % len(e.dsems)]
            e.dnext += 1
            sem, val = slot
            if val > 0:
                self._need(e, ("d", sem, val))
            e.h.dma_start(out=out, in_=in_).then_inc(sem, 16)
            slot[1] = val + 16
            deps.append(("d", sem, val + 16))
        dep = ("m", tuple(deps))
        for b in writes:
            b.w = dep
            b.r = {}
        for b in reads:
            b.r[("m", deps[0][1].num, deps[0][2])] = dep
        return dep

    def wait_all_dma(self, ename):
        e = self.eng[ename]
        for q in self.eng.values():
            for sem, val in q.dsems:
                if val > 0:
                    self._need(e, ("d", sem, val))


class TB:
    __slots__ = ("t", "b")

    def __init__(self, t, name, psum=False):
        self.t = t
        self.b = Buf(name, psum)


def build_program():
    nc = bass.Bass("TRN2", target_bir_lowering=False)
    dt_in = lambda name, shape: nc.dram_tensor(name, shape, F32, kind="ExternalInput").ap()
    dt_out = lambda name, shape: nc.dram_tensor(name, shape, F32, kind="ExternalOutput").ap()
    x_d = dt_in("x", [NTOK, 1024])
    spool_d = dt_in("spool", [16, 15, 512])
    sgla_d = dt_in("sgla", [16, 4, 64, 128])
    w_in_d = dt_in("w_in", [1024, 2576])
    w_map_d = dt_in("w_map", [4, 128, 128])
    pscale_d = dt_in("pscale", [128, 4])
    w_a2_d = dt_in("w_a2", [16, 256])
    b_a_d = dt_in("b_a", [1, 256])
    ng_d = dt_in("ng", [128, 4])
    w_out_d = dt_in("w_out", [1024, 1024])
    ln_g_d = dt_in("ln_g", [1, 1024])
    ln_b_d = dt_in("ln_b", [1, 1024])
    ident_d = dt_in("ident", [128, 128])
    tri_d = dt_in("tri", [128, 2, 128])
    mask_d = dt_in("mask", [128, 2, 128])
    bands_d = dt_in("bands", [128, 4, 6, 128])
    ind_d = dt_in("ind", [128, 16])
    y_d = dt_out("y", [NTOK, 1024])
    pool_p_d = dt_out("pool_p", [15, 512])
    gla_p_d = dt_out("gla_p", [4, 64, 128])
    pool_s_d = dt_out("pool_s", [16, 15, 512])
    gla_s_d = dt_out("gla_s", [16, 4, 64, 128])

    with ExitStack() as st:
        T = Tracker(nc, st)
        V = nc.vector
        A = nc.scalar
        G = nc.gpsimd
        PE = nc.tensor

        def sb(name, shape, dt=F32):
            return TB(st.enter_context(nc.sbuf_tensor(name, shape, dt)), name)

        def ring(name, n, shape, dt=F32):
            return [sb("%s%d" % (name, i), shape, dt) for i in range(n)]

        pbig = [TB(st.enter_context(nc.psum_tensor("pbig", [128, 1024], F32)), "pbig", True)]
        banks = [TB(st.enter_context(nc.psum_tensor("bank%d" % i, [128, 512], F32)), "bank%d" % i, True) for i in range(6)]
        bank_i = [0]

        def nbank():
            b = banks[bank_i[0] % len(banks)]
            bank_i[0] += 1
            return b

        w_in_bf = sb("w_in_bf", [128, 8, 2576], BF16)
        w_out_bf = sb("w_out_bf", [128, 8, 1024], BF16)
        stage_t = st.enter_context(nc.sbuf_tensor("stage", [128, 2, 1024], F32))
        stage_b = [Buf("stage0"), Buf("stage1")]
        w_map_bf = sb("w_map_bf", [128, 4, 128], BF16)
        w_a2_bf = sb("w_a2_bf", [128, 256], BF16)
        pscale = sb("pscale_sb", [128, 4])
        ng = sb("ng_sb", [128, 4])
        ln_g = sb("ln_g_sb", [128, 1024])
        ln_b = sb("ln_b_sb", [128, 1024])
        ident = sb("ident_sb", [128, 128])
        tri = sb("tri_sb", [128, 2, 128])
        mask = sb("mask_sb", [128, 2, 128])
        bands = sb("bands_bf", [128, 4, 6, 128], BF16)
        ind = sb("ind_sb", [128, 16])
        ones_bf = sb("ones_bf", [128, 128], BF16)
        neghalf = sb("neghalf", [128, 2])
        bufS_bf = sb("bufS_bf", [128, 2, 512], BF16)
        S0b = sb("S0b", [128, 16, 128])
        S0b_h = [Buf("S0b_h0"), Buf("S0b_h1")]
        S = sb("S", [128, 2, 128])
        S_bf = ring("S_bf", 2, [128, 2, 128], BF16)

        NX = 4
        xa = ring("xa", NX, [128, 1024])
        xT = ring("xT", 1, [128, 8, 512], BF16)
        faT = ring("faT", 1, [128, 512], BF16)
        qT = ring("qT", 1, [128, 2, 512])
        kT = ring("kT", 1, [128, 2, 512])
        sgp = ring("sgp", 1, [128, 4, 512])
        sgg = ring("sgg", 1, [128, 4, 512])
        u_bf = ring("u_bf", 5, [128, 512], BF16)
        v_bf = ring("v_bf", 4, [128, 512], BF16)
        pooledT = ring("pooledT", 1, [128, 4, 512], BF16)
        mixT = ring("mixT", 1, [128, 8, 512], BF16)
        mixTg = [Buf("mixTg0")]
        mixTg4 = [Buf("mixTg4_%d" % i) for i in range(4)]
        pooledT_g = [Buf("pooledT_g%d" % i) for i in range(4)]
        expg = ring("expg", 1, [128, 512])
        LtP = ring("LtP", 2, [128, 512])
        E = ring("E", 1, [128, 2, 512])
        Kt32 = ring("Kt32", 1, [128, 2, 512])
        Qpad = ring("Qpad", 1, [128, 2, 2, 512], BF16)
        KtT = ring("KtT", 1, [128, 2, 512], BF16)
        Kh_bf = ring("Kh_bf", 4, [128, 512], BF16)
        AT_bf = ring("AT_bf", 2, [128, 512], BF16)
        osq = ring("osq", 2, [128, 512], BF16)
        yo = ring("yo", 2, [128, 1024])
        stats = ring("stats", 2, [128, 12])
        mv = ring("mv", 2, [128, 2])
        lr = ring("lr", 2, [128, 2])
        Khm = ring("Khm", 2, [128, 4, 256], BF16)

        class View:
            def __init__(self, t, b):
                self.t = t
                self.b = b
        hres1 = View(stage_t[:, 0, 0:1024], stage_b[0])
        lnv1 = View(stage_t[:, 1, 0:512], stage_b[1])
        on1 = View(stage_t[:, 1, 512:1024], stage_b[1])

        st.enter_context(nc.Block())

        T.dma("sp", ident.t[:], ident_d, writes=[ident.b])
        T.op("pool", lambda: G.memset(faT[0].t[:], 0.0), writes=[faT[0].b])
        T.op("pool", lambda: G.memset(faT[0].t[32:64, :], 1.0), writes=[faT[0].b])
        T.op("pool", lambda: G.memset(Qpad[0].t[:], 0.0), writes=[Qpad[0].b])
        for i in range(2):
            T.op("pool", lambda i=i: G.memset(S_bf[i].t[:], 0.0), writes=[S_bf[i].b])
        T.op("pool", lambda: G.memset(S.t[:], 0.0), writes=[S.b])
        for i in range(4):
            T.op("pool", lambda i=i: G.memset(Kh_bf[i].t[:], 0.0), writes=[Kh_bf[i].b])
        T.op("pool", lambda: G.memset(ones_bf.t[:], 1.0), writes=[ones_bf.b])
        T.op("pool", lambda: G.memset(neghalf.t[:], -0.5), writes=[neghalf.b])
        sg_v = sgla_d.rearrange("s (b hp) k v -> hp k s b v", hp=2)

        def load_S0_group(grp, dst):
            dv = dst.t[:].rearrange("p (s b v) -> p s b v", s=4, b=2)
            for hp in range(2):
                for b in range(2):
                    T.dma("sp", dv[hp * 64:(hp + 1) * 64, :, b, :], sg_v[hp, :, grp * 4:(grp + 1) * 4, b, :], writes=[dst.b])
            return dv

        def emit_consts():
            T.dma("sp", tri.t[:], tri_d, writes=[tri.b])
            T.dma("sp", mask.t[:], mask_d, writes=[mask.b])
            T.dma("sp", ind.t[:], ind_d, writes=[ind.b])
            T.dma("act", pscale.t[:], pscale_d, writes=[pscale.b])
            T.dma("act", ng.t[:], ng_d, writes=[ng.b])
            T.dma("act", ln_g.t[:], ln_g_d.partition_broadcast(128), writes=[ln_g.b])
            T.dma("act", ln_b.t[:], ln_b_d.partition_broadcast(128), writes=[ln_b.b])
            def nstage():
                sv, sbuf_, own = stg[stg_i[0] % len(stg)]
                stg_i[0] += 1
                return sv, sbuf_, ([sbuf_] + ([own] if own is not None else []))
            sv, sb_, rd = nstage()
            wmv = sv[:, 0:512].rearrange("p (g e) -> p g e", g=4)
            T.dma("sp", wmv, w_map_d.rearrange("g c e -> c g e"), writes=[sb_])
            T.op("dve", lambda wmv=wmv: V.tensor_copy(out=w_map_bf.t[:], in_=wmv), reads=rd, writes=[w_map_bf.b])
            sv, sb_, rd = nstage()
            wav = sv[:, 0:256]
            T.op("pool", lambda wav=wav: G.memset(wav, 0.0), reads=rd[1:], writes=[sb_])
            T.dma("sp", sv[0:16, 0:256], w_a2_d, writes=[sb_])
            T.dma("sp", sv[32:33, 0:256], b_a_d, writes=[sb_])
            T.op("dve", lambda wav=wav: V.tensor_copy(out=w_a2_bf.t[:], in_=wav), reads=rd, writes=[w_a2_bf.b])
            sv, sb_, rd = nstage()
            T.op("pool", lambda sv=sv: G.memset(sv, 0.0), reads=rd[1:], writes=[sb_])
            for hh in range(2):
                T.dma("sp", sv[0:120, hh * 512:(hh + 1) * 512], spool_d[8 * hh:8 * hh + 8].rearrange("s j c -> (s j) c"), writes=[sb_])
            T.op("dve", lambda sv=sv: V.tensor_copy(out=bufS_bf.t[:], in_=sv.rearrange("p (h c) -> p h c", h=2)), reads=rd, writes=[bufS_bf.b])
            for g in range(4):
                sv, sb_, rd = nstage()
                bv = sv[:, 0:768].rearrange("p (k t) -> p k t", k=6)
                T.dma("sp", bv, bands_d[:, g, :, :], writes=[sb_])
                if g % 2 == 0:
                    T.op("dve", lambda g=g, bv=bv: V.tensor_copy(out=bands.t[:, g, :, :], in_=bv), reads=rd, writes=[bands.b])
                else:
                    T.op("dve", lambda g=g, bv=bv: V.tensor_copy(out=bands.t[:, g, :, :], in_=bv), reads=rd, writes=[bands.b])
            T.dma("act", pool_s_d[:, 0:7, :], spool_d[:, 8:15, :])

        col = {"u": 0, "gp": 512, "q": 1024, "k": 1280, "v": 1536, "gg": 2048, "fa": 2560}
        wg_b = {nm: [Buf("w_in_%s%d" % (nm, i)) for i in range(4)] for nm in ("qk", "fa", "u", "v", "gp", "gg")}
        wo_b = [Buf("w_out_%d" % k) for k in range(8)]
        stg = [(stage_t[:, 0, 0:1024], stage_b[0], None), (stage_t[:, 1, 0:1024], stage_b[1], None)]
        for own in (E[0], Kt32[0]):
            stg.append((own.t[:].rearrange("p a b -> p (a b)"), Buf(own.b.name + "_st"), own.b))
        for own in (yo[0], yo[1]):
            stg.append((own.t[:], Buf(own.b.name + "_st"), own.b))
        stg_i = [0]
        cast_eng = [0]

        def stream_cast(src_ap, shape3, dst_ap, dst_buf, eng):
            sv, sbuf_, own = stg[stg_i[0] % len(stg)]
            stg_i[0] += 1
            a_, b_ = shape3
            v = sv[:, 0:a_ * b_].rearrange("p (a b) -> p a b", a=a_)
            T.dma("sp", v, src_ap, writes=[sbuf_])
            rd = [sbuf_] + ([own] if own is not None else [])
            T.op("dve", lambda: V.tensor_copy(out=dst_ap, in_=v), reads=rd, writes=[dst_buf])

        def load_x(ti):
            T.dma("sp", xa[ti % NX].t[:], x_d[ti * 128:(ti + 1) * 128, :], writes=[xa[ti % NX].b])

        xsched = []
        for j in range(4):
            c = [4 * 0 + j, 4 * 1 + j, 4 * 0 + j, 4 * 2 + j, 4 * 1 + j, 4 * 3 + j, 4 * 2 + j]
            if j == 0:
                c.append(16)
            c.append(4 * 3 + j)
            if j == 0:
                c.append(16)
            xsched.append(c)
        xpos = [0, 0, 0, 0]

        def x_advance(slot):
            cur = xsched[slot][xpos[slot]]
            xpos[slot] += 1
            if xpos[slot] < len(xsched[slot]):
                nxt = xsched[slot][xpos[slot]]
                if nxt != cur:
                    load_x(nxt)

        for ti in range(4):
            load_x(ti)
        w_in_v = w_in_d.rearrange("(k p) c -> p k c", p=128)
        groups = {"qk": (1024, 512), "fa": (2560, 16), "u": (0, 512), "v": (1536, 512), "gp": (512, 512), "gg": (2048, 512)}

        def stream_group(nm):
            c0, wd = groups[nm]
            if nm == "fa":
                stream_cast(w_in_v[:, :, c0:c0 + 16], (8, 16), w_in_bf.t[:, :, c0:c0 + 16], wg_b[nm][0], "dve")
                return
            for kk in range(4):
                cast_eng[0] += 1
                stream_cast(w_in_v[:, 2 * kk:2 * kk + 2, c0:c0 + wd], (2, wd), w_in_bf.t[:, 2 * kk:2 * kk + 2, c0:c0 + wd], wg_b[nm][kk],
                            "act" if cast_eng[0] % 2 == 0 else "dve")

        def stream_w_out():
            for kc in range(8):
                stream_cast(w_out_d[kc * 128:(kc + 1) * 128, :].rearrange("p (a b) -> p a b", a=1), (1, 1024),
                            w_out_bf.t[:, kc:kc + 1, :], wo_b[kc], "act" if kc % 2 == 0 else "dve")
        wcol_b = {"u": wg_b["u"], "gp": wg_b["gp"], "q": wg_b["qk"], "k": wg_b["qk"], "v": wg_b["v"], "gg": wg_b["gg"], "fa": wg_b["fa"][0:1]}

        sts = [(0, 4), (4, 4), (8, 4), (12, 4), (16, 1)]
        xTb = xT[0]
        go_v = gla_s_d.rearrange("s (b hp) k v -> hp k s b v", hp=2)
        pj_i = [0]
        prologue_mode = [False]

        def pbank():
            bk = banks[pj_i[0] % 2]
            pj_i[0] += 1
            return bk
        ms_i = [0]

        def mbank():
            bk = banks[2 + ms_i[0] % 4]
            ms_i[0] += 1
            return bk
        BK_A, BK_O, BK_S, BK_Q = banks[2], banks[3], banks[4], banks[5]
        c1_early = [None]
        E_b = [Buf("E_b0"), Buf("E_b1")]
        BK_O2 = [banks[3], banks[5]]

        def unit_A(t0, j):
            ti = t0 + j
            xs = xa[ti % NX]

            def tr():
                ins = None
                for kc in range(8):
                    ins = PE.transpose(out=pbig[0].t[:, kc * 128:(kc + 1) * 128], in_=xs.t[:, kc * 128:(kc + 1) * 128], identity=ident.t[:])
                return ins
            T.op("pe", tr, reads=[xs.b, ident.b], writes=[pbig[0].b])
            T.op("act", lambda: A.copy(out=xTb.t[:, :, j * 128:(j + 1) * 128], in_=pbig[0].t[:].rearrange("p (k t) -> p k t", k=8)),
                 reads=[pbig[0].b], writes=[xTb.b])
            x_advance(ti % NX)

        def proj_fm(W, nm, off, m, evac):
            bk = pbank()
            c0 = col[nm] + off

            def f():
                ins = None
                for kc in range(8):
                    ins = PE.matmul(bk.t[0:m, 0:W], lhsT=w_in_bf.t[:, kc, c0:c0 + m], rhs=xTb.t[:, kc, 0:W], start=(kc == 0), stop=(kc == 7))
                return ins
            T.op("pe", f, reads=wcol_b[nm] + [xTb.b], writes=[bk.b])
            evac(bk)

        def unit_fa(W):
            proj_fm(W, "fa", 0, 16, lambda bk: T.op("act", lambda: A.copy(out=faT[0].t[0:16, 0:W], in_=bk.t[0:16, 0:W]), reads=[bk.b], writes=[faT[0].b]))

        def unit_q(W, b):
            proj_fm(W, "q", b * 128, 128, lambda bk: T.op("act", lambda: A.copy(out=qT[0].t[:, b, 0:W], in_=bk.t[:, 0:W]), reads=[bk.b], writes=[qT[0].b]))

        def unit_k(W, b):
            proj_fm(W, "k", b * 128, 128, lambda bk: T.op("act", lambda: A.copy(out=kT[0].t[:, b, 0:W], in_=bk.t[:, 0:W]), reads=[bk.b], writes=[kT[0].b]))

        def unit_gp(W):
            for g in range(4):
                proj_fm(W, "gp", g * 128, 128, lambda bk, g=g: T.op("act", lambda: A.activation(out=sgp[0].t[:, g, 0:W], in_=bk.t[:, 0:W], func=AF.Silu),
                                                                     reads=[bk.b], writes=[sgp[0].b]))

        def unit_gg(W):
            for h in range(4):
                proj_fm(W, "gg", h * 128, 128, lambda bk, h=h: T.op("act", lambda: A.activation(out=sgg[0].t[:, h, 0:W], in_=bk.t[:, 0:W], func=AF.Silu),
                                                                     reads=[bk.b], writes=[sgg[0].b]))
            for h in range(4):
                T.op("dve", lambda h=h: V.tensor_scalar(out=sgg[0].t[:, h, 0:W], in0=sgg[0].t[:, h, 0:W], scalar1=ng.t[:, h:h + 1], scalar2=None, op0=ALU.mult),
                     reads=[sgg[0].b, ng.b], writes=[sgg[0].b])

        def unit_tm(t0, j, nm, samp):
            ti = t0 + j
            bk = pbank()

            def f():
                ins = None
                for kc in range(8):
                    ins = PE.matmul(bk.t[:, 0:512], lhsT=xTb.t[:, kc, j * 128:(j + 1) * 128], rhs=w_in_bf.t[:, kc, col[nm]:col[nm] + 512],
                                    start=(kc == 0), stop=(kc == 7))
                return ins
            T.op("pe", f, reads=wcol_b[nm] + [xTb.b], writes=[bk.b])
            d = (u_bf[ti % 5] if nm == "u" else v_bf[ti % 4])
            T.op("act", lambda: A.copy(out=d.t[:], in_=bk.t[:, 0:512]), reads=[bk.b], writes=[d.b])
            if nm == "u" and (ti == 15 or samp):
                po = yo[0]
                T.op("act", lambda: A.copy(out=po.t[:, 0:512], in_=bk.t[:, 0:512]), reads=[bk.b], writes=[po.b])
                if ti == 15:
                    T.dma("sp", pool_p_d, po.t[113:128, 0:512], reads=[po.b])
                else:
                    for seq in range(16):
                        T.dma("sp", pool_s_d[seq, 7:15, :], po.t[seq * 8:(seq + 1) * 8, 0:512], reads=[po.b])

        def early_units(sti):
            t0, nt = sts[sti]
            W = 128 * nt
            samp = (sti == 4)
            us = [("A", lambda j=j: unit_A(t0, j)) for j in range(nt)]
            us.append(("fa", lambda: unit_fa(W)))
            us += [("q", lambda b=b: unit_q(W, b)) for b in range(2)]
            us += [("k", lambda b=b: unit_k(W, b)) for b in range(2)]
            us += [("u", lambda j=j: unit_tm(t0, j, "u", samp)) for j in range(nt)]
            us.append(("gp", lambda: unit_gp(W)))
            return us

        def late_units(sti):
            t0, nt = sts[sti]
            W = 128 * nt
            samp = (sti == 4)
            return [lambda j=j: unit_tm(t0, j, "v", samp) for j in range(nt)] + [lambda: unit_gg(W)]

        fillers = []
        late_fillers = []

        def lfill(n=1):
            for _ in range(n):
                if late_fillers:
                    late_fillers.pop(0)()

        def fill(n=1, kinds=None):
            for _ in range(n):
                if fillers and (kinds is None or fillers[0][0] in kinds):
                    fillers.pop(0)[1]()

        for nm_ in ("qk", "fa", "u"):
            stream_group(nm_)
        for nm_ in ("gp", "v", "gg"):
            stream_group(nm_)
        emit_consts()
        stream_w_out()
        prologue_mode[0] = True
        eu = early_units(0)
        lu = late_units(0)
        for k_, u_ in eu:
            u_()
        for u_ in lu:
            u_()
        prologue_mode[0] = False

        for sti, (t0, nt) in enumerate(sts):
            W = 128 * nt
            samp = (sti == 4)
            ci = 1 if samp else 0
            if sti + 1 < len(sts):
                fillers.extend(early_units(sti + 1))
            def C_part1(sti):
                t0, nt = sts[sti]
                W = 128 * nt
                samp = (sti == 4)
                cbk = []
                for g in range(4):
                    bk = banks[2 + g]
                    cbk.append(bk)

                    def f(bk=bk, g=g):
                        ins = None
                        for j in range(nt):
                            ti = t0 + j
                            ucur = u_bf[ti % 5]
                            o = bk.t[:, j * 128:(j + 1) * 128]
                            lw = ucur.t[:, g * 128:(g + 1) * 128]
                            if samp:
                                PE.matmul(o, lhsT=lw, rhs=bands.t[:, g, 3, :], start=True, stop=False)
                                PE.matmul(o, lhsT=bufS_bf.t[:, 0, g * 128:(g + 1) * 128], rhs=bands.t[:, g, 4, :], start=False, stop=False)
                                ins = PE.matmul(o, lhsT=bufS_bf.t[:, 1, g * 128:(g + 1) * 128], rhs=bands.t[:, g, 5, :], start=False, stop=True)
                            elif ti == 0:
                                ins = PE.matmul(o, lhsT=lw, rhs=bands.t[:, g, 2, :], start=True, stop=True)
                            else:
                                PE.matmul(o, lhsT=lw, rhs=bands.t[:, g, 0, :], start=True, stop=False)
                                ins = PE.matmul(o, lhsT=u_bf[(ti - 1) % 5].t[:, g * 128:(g + 1) * 128], rhs=bands.t[:, g, 1, :], start=False, stop=True)
                        return ins
                    urd = [bands.b] + [u_bf[(t0 + j) % 5].b for j in range(nt)]
                    if t0 > 0 and not samp:
                        urd.append(u_bf[(t0 - 1) % 5].b)
                    if samp:
                        urd.append(bufS_bf.b)
                    T.op("pe", f, reads=urd, writes=[bk.b])
                    T.op("act", lambda bk=bk, g=g: A.copy(out=pooledT[0].t[:, g, 0:W], in_=bk.t[:, 0:W]), reads=[bk.b], writes=[pooledT_g[g]])
                return cbk
            if c1_early[0] is None:
                cbk = C_part1(sti)
            else:
                cbk = c1_early[0]
                c1_early[0] = None
            lfill(2)
            for g in range(4):
                bk2 = cbk[g]
                T.op("pe", lambda bk2=bk2, g=g: PE.matmul(bk2.t[:, 0:W], lhsT=w_map_bf.t[:, g, :], rhs=pooledT[0].t[:, g, 0:W], start=True, stop=True),
                     reads=[w_map_bf.b, pooledT_g[g]], writes=[bk2.b])
                T.op("dve", lambda bk2=bk2, g=g: V.scalar_tensor_tensor(out=mixT[0].t[:, g, 0:W], in0=bk2.t[:, 0:W], scalar=pscale.t[:, g:g + 1],
                                                                        in1=sgp[0].t[:, g, 0:W], op0=ALU.mult, op1=ALU.mult),
                     reads=[bk2.b, pscale.b, sgp[0].b], writes=[mixT[0].b])
                if g % 2 == 1:
                    lfill(1)
            for jp in range(0, nt, 2):
                npair = min(2, nt - jp)
                wp = 256 * npair
                bk = mbank()

                def fg(bk=bk, jp=jp, npair=npair):
                    ins = None
                    for jj in range(npair):
                        ins = PE.matmul(bk.t[:, jj * 256:(jj + 1) * 256], lhsT=faT[0].t[:, (jp + jj) * 128:(jp + jj + 1) * 128], rhs=w_a2_bf.t[:],
                                        start=True, stop=True)
                    return ins
                T.op("pe", fg, reads=[faT[0].b, w_a2_bf.b], writes=[bk.b])
                lp = LtP[((t0 + jp) % 4) // 2]
                T.op("act", lambda bk=bk, wp=wp: A.activation(out=expg[0].t[:, 0:wp], in_=bk.t[:, 0:wp], func=AF.Exp, scale=-1.0),
                     reads=[bk.b], writes=[expg[0].b])
                T.op("act", lambda lp=lp, wp=wp: A.activation(out=lp.t[:, 0:wp], in_=expg[0].t[:, 0:wp], func=AF.Ln, bias=1.0),
                     reads=[expg[0].b], writes=[lp.b])
            lfill(len(late_fillers))
            fill(2, ("A",))
            for b in range(2):
                bk = mbank()

                def f(bk=bk, b=b):
                    ins = None
                    for j in range(nt):
                        ins = PE.matmul(bk.t[:, j * 128:(j + 1) * 128], lhsT=LtP[((t0 + j) % 4) // 2].t[:, ((t0 + j) % 2) * 256 + b * 128:((t0 + j) % 2) * 256 + (b + 1) * 128], rhs=tri.t[:, ci, :],
                                        start=True, stop=True)
                    return ins
                T.op("pe", f, reads=[tri.b] + [LtP[((t0 + j) % 4) // 2].b for j in range(nt)], writes=[bk.b])
                T.op("act", lambda bk=bk, b=b: A.activation(out=E[0].t[:, b, 0:W], in_=bk.t[:, 0:W], func=AF.Exp), reads=[bk.b], writes=[E[0].b, E_b[b]])
                T.op("act", lambda bk=bk, b=b: A.activation(out=Kt32[0].t[:, b, 0:W], in_=bk.t[:, 0:W], func=AF.Exp, scale=-1.0), reads=[bk.b], writes=[Kt32[0].b])
            for b in range(2):
                for hp in range(2):
                    ps_ = slice(hp * 64, (hp + 1) * 64)
                    T.op("dve", lambda b=b, hp=hp, ps_=ps_: V.scalar_tensor_tensor(out=Qpad[0].t[ps_, b, hp, 0:W], in0=qT[0].t[ps_, b, 0:W], scalar=0.125,
                                                                                  in1=E[0].t[ps_, b, 0:W], op0=ALU.mult, op1=ALU.mult),
                         reads=[qT[0].b, E_b[b]], writes=[Qpad[0].b])
            T.op("pool", lambda: G.tensor_tensor(out=KtT[0].t[:, :, 0:W], in0=kT[0].t[:, :, 0:W], in1=Kt32[0].t[:, :, 0:W], op=ALU.mult),
                 reads=[kT[0].b, Kt32[0].b], writes=[KtT[0].b])
            for b in range(2):
                if samp:
                    ebc = E[0].t[:, b, 7:128:8].unsqueeze(2).to_broadcast([128, 16, 8])
                    kv = Kt32[0].t[:, b, 0:128].rearrange("p (s t) -> p s t", t=8)
                else:
                    ebc = E[0].t[:, b, 127:512:128].unsqueeze(2).to_broadcast([128, 4, 128])
                    kv = Kt32[0].t[:, b, :].rearrange("p (s t) -> p s t", t=128)
                T.op("pool", lambda kv=kv, ebc=ebc: G.tensor_tensor(out=kv, in0=kv, in1=ebc, op=ALU.mult),
                     reads=[Kt32[0].b, E[0].b], writes=[Kt32[0].b])
            T.op("pool", lambda: G.tensor_tensor(out=Kt32[0].t[:, :, 0:W], in0=kT[0].t[:, :, 0:W], in1=Kt32[0].t[:, :, 0:W], op=ALU.mult),
                 reads=[kT[0].b, Kt32[0].b], writes=[Kt32[0].b])
            fill(2, ("A",))
            fill(1, ("fa",))
            fill(2, ("q",))
            for j in range(nt):
                ti = t0 + j
                bk = mbank()

                def f(bk=bk, j=j):
                    ins = None
                    for b in range(2):
                        ins = PE.transpose(out=bk.t[:, b * 128:(b + 1) * 128], in_=Kt32[0].t[:, b, j * 128:(j + 1) * 128], identity=ident.t[:])
                    return ins
                T.op("pe", f, reads=[Kt32[0].b, ident.b], writes=[bk.b])
                if samp:
                    T.op("act", lambda bk=bk, ti=ti: A.copy(out=Kh_bf[ti % 4].t[:, 0:256], in_=bk.t[:, 0:256]), reads=[bk.b], writes=[Kh_bf[ti % 4].b])
                else:
                    kdst = bass.AP(Kh_bf[ti % 4].t, 0, [[512, 128], [256, 2], [192, 2], [1, 64]])
                    ksrc = bass.AP(bk.t, 0, [[512, 128], [128, 2], [64, 2], [1, 64]])
                    T.op("act", lambda kdst=kdst, ksrc=ksrc: A.copy(out=kdst, in_=ksrc), reads=[bk.b], writes=[Kh_bf[ti % 4].b])
            lfill(len(late_fillers))
            def E1a(j):
                ti = t0 + j
                tsl = slice(j * 128, (j + 1) * 128)
                a2 = ti % 2

                def fA():
                    ins = None
                    for b in range(2):
                        ins = PE.matmul(BK_A.t[:, b * 256:(b + 1) * 256].rearrange("p (h t) -> p h t", h=2), lhsT=KtT[0].t[:, b, tsl],
                                        rhs=Qpad[0].t[:, b, :, tsl], start=True, stop=True)
                    return ins
                T.op("pe", fA, reads=[KtT[0].b, Qpad[0].b], writes=[BK_A.b])
                T.op("dve", lambda: V.tensor_tensor(out=AT_bf[a2].t[:].rearrange("p (h t) -> p h t", h=4),
                                                    in0=BK_A.t[:, 0:512].rearrange("p (h t) -> p h t", h=4),
                                                    in1=mask.t[:, 0, :].unsqueeze(1).to_broadcast([128, 4, 128]), op=ALU.mult),
                     reads=[BK_A.b, mask.b], writes=[AT_bf[a2].b])

            def E1b(j):
                ti = t0 + j
                tsl = slice(j * 128, (j + 1) * 128)
                a2 = ti % 2
                bkO = BK_O2[ti % 2]
                sprev = S_bf[ti % 2]
                snext = S_bf[(ti + 1) % 2]

                def fO():
                    ins = None
                    for b in range(2):
                        ob = bkO.t[:, b * 256:(b + 1) * 256].rearrange("p (h t) -> p h t", h=2)
                        PE.matmul(ob, lhsT=sprev.t[:, b, :], rhs=Qpad[0].t[:, b, :, tsl], start=(b == 0), stop=False, skip_group_check=True)
                    for h in range(4):
                        ins = PE.matmul(bkO.t[:, h * 128:(h + 1) * 128], lhsT=v_bf[ti % 4].t[:, h * 128:(h + 1) * 128],
                                        rhs=AT_bf[a2].t[:, h * 128:(h + 1) * 128], start=False, stop=True, skip_group_check=True)
                    return ins
                T.op("pe", fO, reads=[Qpad[0].b, AT_bf[a2].b, v_bf[ti % 4].b, sprev.b], writes=[bkO.b])

                def fS():
                    ins = None
                    for b in range(2):
                        for hp in range(2):
                            ins = PE.matmul(BK_S.t[:, b * 128:(b + 1) * 128], lhsT=Kh_bf[ti % 4].t[:, (2 * b + hp) * 128:(2 * b + hp + 1) * 128],
                                            rhs=v_bf[ti % 4].t[:, (2 * b + hp) * 128:(2 * b + hp + 1) * 128], start=(hp == 0), stop=(hp == 1))
                    return ins
                T.op("pe", fS, reads=[Kh_bf[ti % 4].b, v_bf[ti % 4].b], writes=[BK_S.b])
                for b in range(2):
                    T.op("dve", lambda b=b: V.scalar_tensor_tensor(
                        out=S.t[:, b, :], in0=S.t[:, b, :], scalar=E[0].t[:, b, j * 128 + 127:j * 128 + 128],
                        in1=BK_S.t[:, b * 128:(b + 1) * 128], op0=ALU.mult, op1=ALU.add),
                        reads=[S.b, E[0].b, BK_S.b], writes=[S.b])
                T.op("act", lambda: A.copy(out=snext.t[:], in_=S.t[:]), reads=[S.b], writes=[snext.b])
                if ti == 15:
                    for b in range(2):
                        for hp in range(2):
                            T.dma("sp", gla_p_d[2 * b + hp], S.t[hp * 64:(hp + 1) * 64, b, :], reads=[S.b])

            def E2a(j):
                ti = t0 + j
                a2 = ti % 2
                bkO = BK_O2[ti % 2]
                T.op("act", lambda: A.activation(out=osq[a2].t[:], in_=bkO.t[:, 0:512], func=AF.Square), reads=[bkO.b], writes=[osq[a2].b])

            def E2b_pe(j):
                ti = t0 + j
                a2 = ti % 2
                bkQ = BK_A
                T.op("pe", lambda: PE.matmul(bkQ.t[:, 0:512], lhsT=ones_bf.t[:], rhs=osq[a2].t[:], start=True, stop=True),
                     reads=[ones_bf.b, osq[a2].b], writes=[bkQ.b])
                T.op("act", lambda: A.activation(out=lnv1.t, in_=bkQ.t[:, 0:512], func=AF.Ln, scale=1.0 / 128.0, bias=RMS_EPS),
                     reads=[bkQ.b], writes=[lnv1.b])

            def E2b(j):
                ti = t0 + j
                tsl = slice(j * 128, (j + 1) * 128)
                a2 = ti % 2
                bkO = BK_O2[ti % 2]
                bkQ = BK_A
                T.op("act", lambda: A.activation(out=lnv1.t, in_=lnv1.t, func=AF.Exp, scale=-0.5), reads=[lnv1.b], writes=[lnv1.b])
                T.op("dve", lambda: V.tensor_tensor(out=on1.t, in0=bkO.t[:, 0:512], in1=lnv1.t, op=ALU.mult),
                     reads=[bkO.b, lnv1.b], writes=[on1.b])
                T.op("pool", lambda: G.tensor_tensor(out=mixT[0].t[:, 4:8, tsl], in0=on1.t.rearrange("p (h t) -> p h t", h=4),
                                                      in1=sgg[0].t[:, :, tsl], op=ALU.mult),
                     reads=[on1.b, sgg[0].b], writes=[mixTg4[j]])

            def Fst_pe(j):
                tsl = slice(j * 128, (j + 1) * 128)

                def fY():
                    ins = None
                    for n in range(2):
                        for e in range(8):
                            ins = PE.matmul(pbig[0].t[:, n * 512:(n + 1) * 512], lhsT=mixT[0].t[:, e, tsl], rhs=w_out_bf.t[:, e, n * 512:(n + 1) * 512],
                                            start=(e == 0), stop=(e == 7))
                    return ins
                T.op("pe", fY, reads=[mixT[0].b, mixTg4[j]] + wo_b, writes=[pbig[0].b])

            def Fst(j):
                ti = t0 + j
                a2 = ti % 2
                xrs = xa[ti % NX]
                T.op("dve", lambda: V.scalar_tensor_tensor(out=hres1.t, in0=xrs.t[:], scalar=ALPHA, in1=pbig[0].t[:], op0=ALU.mult, op1=ALU.add),
                     reads=[xrs.b, pbig[0].b], writes=[hres1.b])
                x_advance(ti % NX)
                T.op("dve", lambda: V.bn_stats(out=stats[a2].t[:, 0:6], in_=hres1.t[:, 0:512]), reads=[hres1.b], writes=[stats[a2].b])
                T.op("dve", lambda: V.bn_stats(out=stats[a2].t[:, 6:12], in_=hres1.t[:, 512:1024]), reads=[hres1.b], writes=[stats[a2].b])
                T.op("dve", lambda: V.bn_aggr(out=mv[a2].t[:], in_=stats[a2].t[:]), reads=[stats[a2].b], writes=[mv[a2].b])
                T.op("pool", lambda: G.tensor_scalar(out=lr[a2].t[:, 0:1], in0=mv[a2].t[:, 1:2], scalar1=LN_EPS, scalar2=None, op0=ALU.add),
                     reads=[mv[a2].b], writes=[lr[a2].b])
                T.op("pool", lambda: G.tensor_tensor(out=lr[a2].t[:, 0:1], in0=lr[a2].t[:, 0:1], in1=neghalf.t[:, 0:1], op=ALU.pow),
                     reads=[lr[a2].b, neghalf.b], writes=[lr[a2].b])
                T.op("dve", lambda: V.tensor_scalar(out=yo[a2].t[:], in0=hres1.t, scalar1=mv[a2].t[:, 0:1], scalar2=lr[a2].t[:, 0:1],
                                                    op0=ALU.subtract, op1=ALU.mult),
                     reads=[hres1.b, mv[a2].b, lr[a2].b], writes=[yo[a2].b])
                T.op("dve", lambda: V.tensor_tensor(out=yo[a2].t[:], in0=yo[a2].t[:], in1=ln_g.t[:], op=ALU.mult), reads=[yo[a2].b, ln_g.b], writes=[yo[a2].b])
                T.op("pool", lambda: G.tensor_tensor(out=yo[a2].t[:], in0=yo[a2].t[:], in1=ln_b.t[:], op=ALU.add), reads=[yo[a2].b, ln_b.b], writes=[yo[a2].b])
                T.dma("sp", y_d[ti * 128:(ti + 1) * 128, :], yo[a2].t[:], reads=[yo[a2].b])

            if not samp:
                E1a(0)
                fill(1)
                E1b(0)
                E2a(0)
                for j in range(nt):
                    if j >= 1:
                        Fst_pe(j - 1)
                    E2b_pe(j)
                    if j + 1 < nt:
                        E1a(j + 1)
                    E2b(j)
                    if j in (0, 1, 2):
                        fill(1)
                    if j + 1 < nt:
                        E1b(j + 1)
                        E2a(j + 1)
                    if j in (0, 1, 3):
                        fill(1)
                    if j >= 1:
                        Fst(j - 1)
                if sti + 1 < len(sts):
                    while fillers:
                        fillers.pop(0)[1]()
                    c1_early[0] = C_part1(sti + 1)
                Fst_pe(nt - 1)
                Fst(nt - 1)
            if samp:
                ti = 16
                a2 = 0
                tsl = slice(0, 128)
                S0blk = [S0b.t[:], sgp[0].t[:].rearrange("p g (s v) -> p (g s) v", v=128)]
                S0blk_b = [S0b_h[0], sgp[0].b]
                for b in range(2):
                    T.dma_batch("sp", [(S0blk[b][hp * 64:(hp + 1) * 64, :, :], sg_v[hp, :, :, b, :]) for hp in range(2)], writes=[S0blk_b[b]])
                gbuf = [yo[0], yo[1], xa[1], xa[2]]
                gdv = [g_.t[:].rearrange("p (s b v) -> p s b v", s=4, b=2) for g_ in gbuf]
                for grp in range(4):
                    T.dma_batch("sp", [(gdv[grp][hp * 64:(hp + 1) * 64, :, b, :], sg_v[hp, :, grp * 4:(grp + 1) * 4, b, :])
                                       for hp in range(2) for b in range(2)], writes=[gbuf[grp].b])
                xTv = xT[0].t[:].rearrange("p k t -> p (k t)")
                S0bf = [xTv[:, b * 2048:(b + 1) * 2048].rearrange("p (s v) -> p s v", v=128) for b in range(2)]
                for b in range(2):
                    T.op("act", lambda b=b: A.copy(out=S0bf[b], in_=S0blk[b]), reads=[S0blk_b[b]], writes=[xT[0].b])
                kh = Kh_bf[16 % 4]
                qbs = [[Buf("sq_%d_%d" % (g_, q_)) for q_ in range(4)] for g_ in range(4)]
                for g_ in range(4):
                    ebc = bass.AP(E[0].t, g_ * 32 + 7, [[1024, 128], [8, 4], [512, 2], [0, 128]])
                    T.op("pool", lambda dv=gdv[g_], ebc=ebc: G.tensor_tensor(out=dv, in0=dv, in1=ebc, op=ALU.mult),
                         reads=[gbuf[g_].b, E[0].b], writes=[gbuf[g_].b] + qbs[g_])

                def supd(grp):
                    km = Khm[grp % 2]
                    T.op("dve", lambda km=km, grp=grp: V.tensor_tensor(out=km.t[:], in0=kh.t[:, 0:256].unsqueeze(1).to_broadcast([128, 4, 256]),
                                                                       in1=ind.t[:, grp * 4:(grp + 1) * 4].unsqueeze(2).to_broadcast([128, 4, 256]), op=ALU.mult),
                         reads=[kh.b, ind.b], writes=[km.b])
                    sdst = gbuf[grp]
                    dv = gdv[grp]
                    qb = qbs[grp]
                    for qq in range(4):
                        bk = mbank()

                        def fS(bk=bk, qq=qq, km=km):
                            ins = None
                            for b in range(2):
                                ins = PE.matmul(bk.t[:, b * 256:(b + 1) * 256], lhsT=km.t[:, qq, b * 128:(b + 1) * 128],
                                                rhs=v_bf[16 % 4].t[:, b * 256:(b + 1) * 256], start=True, stop=True)
                            return ins
                        T.op("pe", fS, reads=[km.b, v_bf[16 % 4].b], writes=[bk.b])
                        bv = bk.t[:, 0:512].rearrange("p (b h v) -> p b h v", b=2, h=2)
                        for hp in range(2):
                            ps_ = slice(hp * 64, (hp + 1) * 64)
                            T.op("dve", lambda bv=bv, qq=qq, hp=hp, ps_=ps_, dv=dv: V.tensor_tensor(
                                out=dv[ps_, qq, :, :], in0=dv[ps_, qq, :, :], in1=bv[ps_, :, hp, :], op=ALU.add),
                                reads=[bk.b], writes=[qb[qq]] if hp == 0 else [qb[qq]])
                    T.dma_batch("sp", [(go_v[hp, :, grp * 4:(grp + 1) * 4, b, :], dv[hp * 64:(hp + 1) * 64, :, b, :])
                                       for hp in range(2) for b in range(2)], reads=qb + [sdst.b])

                def fA():
                    ins = None
                    for b in range(2):
                        ins = PE.matmul(BK_A.t[:, b * 256:(b + 1) * 256].rearrange("p (h t) -> p h t", h=2), lhsT=KtT[0].t[:, b, tsl],
                                        rhs=Qpad[0].t[:, b, :, tsl], start=True, stop=True)
                    return ins
                T.op("pe", fA, reads=[KtT[0].b, Qpad[0].b], writes=[BK_A.b])
                T.op("dve", lambda: V.tensor_tensor(out=AT_bf[a2].t[:].rearrange("p (h t) -> p h t", h=4),
                                                    in0=BK_A.t[:, 0:512].rearrange("p (h t) -> p h t", h=4),
                                                    in1=mask.t[:, 1, :].unsqueeze(1).to_broadcast([128, 4, 128]), op=ALU.mult),
                     reads=[BK_A.b, mask.b], writes=[AT_bf[a2].b])
                supd(0)
                bkO = BK_O

                def fO():
                    ins = None
                    first = True
                    for b in range(2):
                        for seq in range(16):
                            for hp in range(2):
                                c0 = b * 256 + hp * 128 + seq * 8
                                PE.matmul(bkO.t[:, c0:c0 + 8], lhsT=S0bf[b][:, seq, :], rhs=Qpad[0].t[:, b, hp, seq * 8:(seq + 1) * 8],
                                          start=first, stop=False, skip_group_check=True)
                                first = False
                    for h in range(4):
                        ins = PE.matmul(bkO.t[:, h * 128:(h + 1) * 128], lhsT=v_bf[16 % 4].t[:, h * 128:(h + 1) * 128],
                                        rhs=AT_bf[a2].t[:, h * 128:(h + 1) * 128], start=False, stop=True, skip_group_check=True)
                    return ins
                T.op("pe", fO, reads=[Qpad[0].b, xT[0].b, AT_bf[a2].b, v_bf[16 % 4].b], writes=[bkO.b])
                T.op("act", lambda: A.activation(out=osq[a2].t[:], in_=bkO.t[:, 0:512], func=AF.Square), reads=[bkO.b], writes=[osq[a2].b])
                bkQ = BK_Q
                T.op("pe", lambda: PE.matmul(bkQ.t[:, 0:512], lhsT=ones_bf.t[:], rhs=osq[a2].t[:], start=True, stop=True),
                     reads=[ones_bf.b, osq[a2].b], writes=[bkQ.b])
                T.op("act", lambda: A.activation(out=lnv1.t, in_=bkQ.t[:, 0:512], func=AF.Ln, scale=1.0 / 128.0, bias=RMS_EPS),
                     reads=[bkQ.b], writes=[lnv1.b])
                T.op("act", lambda: A.activation(out=lnv1.t, in_=lnv1.t, func=AF.Exp, scale=-0.5), reads=[lnv1.b], writes=[lnv1.b])
                T.op("dve", lambda: V.tensor_tensor(out=on1.t, in0=bkO.t[:, 0:512], in1=lnv1.t, op=ALU.mult),
                     reads=[bkO.b, lnv1.b], writes=[on1.b])
                T.op("dve", lambda: V.tensor_tensor(out=mixT[0].t[:, 4:8, tsl], in0=on1.t.rearrange("p (h t) -> p h t", h=4),
                                                    in1=sgg[0].t[:, :, tsl], op=ALU.mult),
                     reads=[on1.b, sgg[0].b], writes=[mixTg[0], mixTg4[0]])
                supd(1)
                xrs = xa[0]
                yb = xa[3]

                def fY():
                    ins = None
                    for n in range(2):
                        for e in range(8):
                            ins = PE.matmul(pbig[0].t[:, n * 512:(n + 1) * 512], lhsT=mixT[0].t[:, e, tsl], rhs=w_out_bf.t[:, e, n * 512:(n + 1) * 512],
                                            start=(e == 0), stop=(e == 7))
                    return ins
                T.op("pe", fY, reads=[mixT[0].b, mixTg[0]] + wo_b, writes=[pbig[0].b])
                supd(2)
                T.op("dve", lambda: V.scalar_tensor_tensor(out=hres1.t, in0=xrs.t[:], scalar=ALPHA, in1=pbig[0].t[:], op0=ALU.mult, op1=ALU.add),
                     reads=[xrs.b, pbig[0].b], writes=[hres1.b])
                T.op("dve", lambda: V.bn_stats(out=stats[a2].t[:, 0:6], in_=hres1.t[:, 0:512]), reads=[hres1.b], writes=[stats[a2].b])
                T.op("dve", lambda: V.bn_stats(out=stats[a2].t[:, 6:12], in_=hres1.t[:, 512:1024]), reads=[hres1.b], writes=[stats[a2].b])
                T.op("dve", lambda: V.bn_aggr(out=mv[a2].t[:], in_=stats[a2].t[:]), reads=[stats[a2].b], writes=[mv[a2].b])
                T.op("pool", lambda: G.tensor_scalar(out=lr[a2].t[:, 0:1], in0=mv[a2].t[:, 1:2], scalar1=LN_EPS, scalar2=None, op0=ALU.add),
                     reads=[mv[a2].b], writes=[lr[a2].b])
                T.op("pool", lambda: G.tensor_tensor(out=lr[a2].t[:, 0:1], in0=lr[a2].t[:, 0:1], in1=neghalf.t[:, 0:1], op=ALU.pow),
                     reads=[lr[a2].b, neghalf.b], writes=[lr[a2].b])
                T.op("dve", lambda: V.scalar_tensor_tensor(out=lr[a2].t[:, 1:2], in0=mv[a2].t[:, 0:1], scalar=-1.0, in1=lr[a2].t[:, 0:1],
                                                           op0=ALU.mult, op1=ALU.mult),
                     reads=[mv[a2].b, lr[a2].b], writes=[lr[a2].b])
                T.op("dve", lambda: V.tensor_scalar(out=yb.t[:], in0=hres1.t, scalar1=lr[a2].t[:, 0:1], scalar2=lr[a2].t[:, 1:2],
                                                    op0=ALU.mult, op1=ALU.add),
                     reads=[hres1.b, lr[a2].b], writes=[yb.b])
                T.op("pool", lambda: G.tensor_tensor(out=yb.t[:], in0=yb.t[:], in1=ln_g.t[:], op=ALU.mult), reads=[yb.b, ln_g.b], writes=[yb.b])
                T.op("pool", lambda: G.tensor_tensor(out=yb.t[:], in0=yb.t[:], in1=ln_b.t[:], op=ALU.add), reads=[yb.b, ln_b.b], writes=[yb.b])
                T.dma("sp", y_d[ti * 128:(ti + 1) * 128, :], yb.t[:], reads=[yb.b])
                supd(3)
            while fillers:
                fillers.pop(0)[1]()
            if sti + 1 < len(sts):
                late_fillers.extend(late_units(sti + 1))
        T.wait_all_dma("sp")
    return nc


_CACHE = {}


def kernel(x_prompt, x_sample, state_pool, state_gla, w_in, w_pool_map, pool_scale, w_gla_a2,
           b_gla_a, gla_norm_g, w_out, ln_g, ln_b):
    f = lambda a: np.ascontiguousarray(np.asarray(a, dtype=np.float32))
    x_prompt, x_sample, state_pool, state_gla = f(x_prompt), f(x_sample), f(state_pool), f(state_gla)
    if "nc" not in _CACHE:
        _CACHE["nc"] = build_program()
        _CACHE["consts"] = build_consts()
    nc = _CACHE["nc"]
    C = _CACHE["consts"]
    shared = {
        "w_in": f(w_in)[0], "w_map": f(w_pool_map)[0], "pscale": np.ascontiguousarray(f(pool_scale)[0].reshape(4, 128).T), "w_a2": f(w_gla_a2)[0],
        "b_a": f(b_gla_a)[0].reshape(1, 256), "ng": np.ascontiguousarray(f(gla_norm_g)[0].reshape(4, 128).T), "w_out": f(w_out)[0],
        "ln_g": f(ln_g)[0].reshape(1, 1024), "ln_b": f(ln_b)[0].reshape(1, 1024),
        "ident": C["ident"], "tri": C["tri"], "mask": C["mask"], "bands": C["bands"], "ind": C["ind"],
    }
    in_maps = []
    for c in range(NCORES):
        m = dict(shared)
        m["x"] = np.ascontiguousarray(np.concatenate([x_prompt[c], x_sample[16 * c:16 * c + 16].reshape(128, 1024)], axis=0))
        m["spool"] = np.ascontiguousarray(state_pool[0, 16 * c:16 * c + 16])
        m["sgla"] = np.ascontiguousarray(state_gla[0, 16 * c:16 * c + 16])
        in_maps.append(m)
    res = run_bass_kernel_spmd(nc, in_maps, core_ids=list(range(NCORES)))
    R = res.results
    y_prompt = np.stack([R[c]["y"][:2048] for c in range(NCORES)], 0).astype(np.float32)
    y_sample = np.concatenate([R[c]["y"][2048:].reshape(16, 8, 1024) for c in range(NCORES)], 0).astype(np.float32)
    pool_p = np.stack([R[c]["pool_p"] for c in range(NCORES)], 0)[None].astype(np.float32)
    gla_p = np.stack([R[c]["gla_p"] for c in range(NCORES)], 0)[None].astype(np.float32)
    pool_s = np.concatenate([R[c]["pool_s"] for c in range(NCORES)], 0)[None].astype(np.float32)
    gla_s = np.concatenate([R[c]["gla_s"] for c in range(NCORES)], 0)[None].astype(np.float32)
    return (y_prompt, y_sample, pool_p, gla_p, pool_s, gla_s)
```
